# Optimizing a Trainium2 kernel written in Bass

```python
import math
import jax
import jax.numpy as jnp
from jax import lax
import numpy as np

D_MODEL = 1024
BATCH = 4
SEQ = 4096
DEPTH = 1

GRID_W = 64
CTX_LEN = 256
D_MIX = D_MODEL

HY_WIDTH = D_MIX // 2
HY_ORDER = 2
HY_SHORT = 3
HY_BANDS = 8
HY_POS_DIM = 1 + 2 * HY_BANDS
HY_FILTER_HIDDEN = 64
HY_FILTER_SCALE = 0.05
HY_DECAY_TARGET = 1e-2
HY_FAST_DECAY_PCT = 0.3
HY_SLOW_DECAY_PCT = 1.5
HY_WINDOW_SHIFT = 0.05

ATT_HEADS = 8
ATT_KV_HEADS = 2
ATT_REP = ATT_HEADS // ATT_KV_HEADS
HEAD_DIM = 64
ATT_WIDTH = ATT_HEADS * HEAD_DIM
WINDOW = 128
BLOCK = 128
ROPE_BASE = 10000.0
ROPE_FREQS = HEAD_DIM // 4

PROJ_HY = (HY_ORDER + 1) * HY_WIDTH
PROJ_Q = ATT_WIDTH
PROJ_KV = ATT_KV_HEADS * HEAD_DIM
KV_START = PROJ_HY + PROJ_Q
PROJ_TOTAL = KV_START + 2 * PROJ_KV

PEER_KEYS = 128
PEER_EXPERTS = PEER_KEYS * PEER_KEYS
PEER_HEADS = 8
PEER_QDIM = 256
PEER_TOPK = 16
PEER_BLOCK = 128

LN_EPS = 1e-5
NEG_INF = -1e30
DEEPNORM_ALPHA = (2.0 * DEPTH) ** 0.25
DEEPNORM_BETA = (8.0 * DEPTH) ** -0.25

kernel_name = 'hybrid_hyena_swa_peer_diffusion_block'


def layer_norm(x, g, b):
    xf = x.astype(jnp.float32)
    mu = jnp.mean(xf, axis=-1, keepdims=True)
    var = jnp.mean(jnp.square(xf - mu), axis=-1, keepdims=True)
    return ((xf - mu) * lax.rsqrt(var + LN_EPS)).astype(x.dtype) * g + b


def short_conv(u, w, b):
    n = u.shape[1]
    up = jnp.pad(u, ((0, 0), (1, 1), (0, 0)))
    return up[:, :n] * w[0] + up[:, 1:n + 1] * w[1] + up[:, 2:] * w[2] + b


def hyena_filters(n, w1, b1, f1, w2, b2, f2, w3, b3):
    f32 = jnp.float32
    t = jnp.linspace(0.0, 1.0, n, dtype=f32)[:, None]
    w = 2.0 * math.pi * jnp.arange(n, dtype=f32)[:, None] / n
    bands = jnp.linspace(1e-4, HY_BANDS - 1, HY_BANDS, dtype=f32)[None, :]
    z = jnp.concatenate([t, jnp.cos(bands * w), -jnp.sin(bands * w)], axis=-1)
    h = jnp.sin(f1.astype(f32) * (z @ w1.astype(f32) + b1.astype(f32)))
    h = jnp.sin(f2.astype(f32) * (h @ w2.astype(f32) + b2.astype(f32)))
    h = (h @ w3.astype(f32) + b3.astype(f32)).reshape(n, 2, HY_ORDER, HY_WIDTH)
    min_decay = math.log(HY_DECAY_TARGET) / HY_SLOW_DECAY_PCT
    max_decay = math.log(HY_DECAY_TARGET) / HY_FAST_DECAY_PCT
    deltas = jnp.abs(jnp.linspace(min_decay, max_decay, HY_WIDTH, dtype=f32))
    window = jnp.exp(-t * deltas[None, :])[:, None, None, :]
    return h * (window + HY_WINDOW_SHIFT)


def two_sided_spectrum(h):
    n = h.shape[0]
    taps = jnp.concatenate([h[:, 0], jnp.zeros_like(h[:1, 0]), h[1:, 1][::-1]], axis=0)
    return jnp.fft.rfft(taps, axis=0)


def fft_long_conv(u, spec, skip):
    n = u.shape[1]
    y = jnp.fft.irfft(jnp.fft.rfft(u, n=2 * n, axis=1) * spec[None], n=2 * n, axis=1)[:, :n]
    return y + u * skip


def hyena_mixer(p, conv_w, conv_b, w1, b1, f1, w2, b2, f2, w3, b3, skip):
    n = p.shape[1]
    u = short_conv(p, conv_w, conv_b).astype(jnp.float32)
    v, x1, x2 = jnp.split(u, HY_ORDER + 1, axis=-1)
    spec = two_sided_spectrum(hyena_filters(n, w1, b1, f1, w2, b2, f2, w3, b3))
    skip = skip.astype(jnp.float32)
    y = x1 * fft_long_conv(v, spec[:, 0], skip[0])
    y = x2 * fft_long_conv(y, spec[:, 1], skip[1])
    return y.astype(p.dtype)


def axial_rope_tables(n):
    rows = n // GRID_W
    row = jnp.repeat(jnp.arange(rows, dtype=jnp.float32), GRID_W)
    col = jnp.tile(jnp.arange(GRID_W, dtype=jnp.float32), rows)
    inv = ROPE_BASE ** (-jnp.arange(ROPE_FREQS, dtype=jnp.float32) / ROPE_FREQS)
    ang = jnp.stack([row[:, None] * inv, col[:, None] * inv], axis=1)
    return jnp.cos(ang), jnp.sin(ang)


def apply_axial_rope(t, cos, sin):
    xs = t.astype(jnp.float32).reshape(t.shape[:-1] + (2, 2, ROPE_FREQS))
    a, b = xs[..., 0, :], xs[..., 1, :]
    cos, sin = cos[None, :, None], sin[None, :, None]
    out = jnp.stack([a * cos - b * sin, b * cos + a * sin], axis=-2)
    return out.reshape(t.shape).astype(t.dtype)


def band_blocks(t):
    b, n = t.shape[:2]
    nb = n // BLOCK
    tp = jnp.pad(t, ((0, 0), (BLOCK, BLOCK), (0, 0), (0, 0))).reshape((b, nb + 2, BLOCK) + t.shape[2:])
    return jnp.concatenate([tp[:, :-2], tp[:, 1:-1], tp[:, 2:]], axis=2)


def latent_attention(q, k, v, k_ctx, v_ctx, sink):
    b, n = q.shape[:2]
    nb = n // BLOCK
    n_loc = 3 * BLOCK
    n_ctx = k_ctx.shape[1]
    scale = HEAD_DIM ** -0.5
    qb = q.reshape(b, nb, BLOCK, ATT_KV_HEADS, ATT_REP, HEAD_DIM)
    kb, vb = band_blocks(k), band_blocks(v)
    s_loc = jnp.einsum('bnqgrd,bnkgd->bngrqk', qb, kb).astype(jnp.float32) * scale
    s_ctx = jnp.einsum('bnqgrd,bcgd->bngrqc', qb, k_ctx).astype(jnp.float32) * scale
    qi = jnp.arange(BLOCK)[:, None]
    kj = jnp.arange(n_loc)[None, :]
    kpos = (jnp.arange(nb)[:, None, None] - 1) * BLOCK + kj[None]
    mask = (jnp.abs(kj - BLOCK - qi)[None] <= WINDOW) & (kpos >= 0) & (kpos < n)
    s_loc = jnp.where(mask[None, :, None, None], s_loc, NEG_INF)
    s_sink = jnp.broadcast_to(sink.astype(jnp.float32).reshape(1, 1, ATT_KV_HEADS, ATT_REP, 1, 1), s_loc.shape[:-1] + (1,))
    p = jax.nn.softmax(jnp.concatenate([s_loc, s_ctx, s_sink], axis=-1), axis=-1).astype(v.dtype)
    o = (jnp.einsum('bngrqk,bnkgd->bnqgrd', p[..., :n_loc], vb)
         + jnp.einsum('bngrqc,bcgd->bnqgrd', p[..., n_loc:n_loc + n_ctx], v_ctx))
    return o.reshape(b, n, ATT_WIDTH)


def context_attention(q, k, v, sink):
    b, n = q.shape[:2]
    qg = q.reshape(b, n, ATT_KV_HEADS, ATT_REP, HEAD_DIM)
    s = jnp.einsum('bqgrd,bkgd->bgrqk', qg, k).astype(jnp.float32) * HEAD_DIM ** -0.5
    s_sink = jnp.broadcast_to(sink.astype(jnp.float32).reshape(1, ATT_KV_HEADS, ATT_REP, 1, 1), s.shape[:-1] + (1,))
    p = jax.nn.softmax(jnp.concatenate([s, s_sink], axis=-1), axis=-1).astype(v.dtype)
    o = jnp.einsum('bgrqk,bkgd->bqgrd', p[..., :n], v)
    return o.reshape(b, n, ATT_WIDTH)


def peer_mixer(h, wq, keys1, keys2, u_tab, v_tab):
    b, n, d = h.shape
    nb = n // PEER_BLOCK
    half = PEER_QDIM // 2
    hb = jnp.swapaxes(h.reshape(b, nb, PEER_BLOCK, d), 0, 1)

    def retrieve(xb):
        q = (xb @ wq).reshape(b, PEER_BLOCK, PEER_HEADS, 2, half)
        s1 = jnp.einsum('bthd,kd->bthk', q[..., 0, :], keys1).astype(jnp.float32)
        s2 = jnp.einsum('bthd,kd->bthk', q[..., 1, :], keys2).astype(jnp.float32)
        v1, i1 = lax.top_k(s1, PEER_TOPK)
        v2, i2 = lax.top_k(s2, PEER_TOPK)
        cand_shape = (b, PEER_BLOCK, PEER_HEADS, PEER_TOPK * PEER_TOPK)
        cand_s = (v1[..., :, None] + v2[..., None, :]).reshape(cand_shape)
        cand_i = (i1[..., :, None] * PEER_KEYS + i2[..., None, :]).reshape(cand_shape)
        top_s, pos = lax.top_k(cand_s, PEER_TOPK)
        experts = jnp.take_along_axis(cand_i, pos, axis=-1)
        g = jax.nn.softmax(top_s, axis=-1).astype(xb.dtype)
        act = jax.nn.gelu(jnp.einsum('bthkd,btd->bthk', u_tab[experts], xb), approximate=False)
        return jnp.einsum('bthk,bthkd->btd', act * g, v_tab[experts])

    y = lax.map(retrieve, hb)
    return jnp.swapaxes(y, 0, 1).reshape(b, n, d)


def setup_inputs(seed: int = 0) -> dict:
    key = jax.random.key(seed)
    ks = jax.random.split(key, 32)
    f32 = jnp.float32

    def nrm(k, shape, std):
        return jax.random.normal(k, shape, f32) * std

    L = DEPTH
    return {
        'x': nrm(ks[0], (BATCH, SEQ, D_MODEL), 1.0),
        'c': nrm(ks[1], (BATCH, D_MODEL), 1.0),
        'ctx': nrm(ks[2], (BATCH, CTX_LEN, D_MODEL), 1.0),
        'c_ctx': nrm(ks[3], (D_MODEL,), 1.0),
        'w_mod': nrm(ks[4], (L, D_MODEL, 6 * D_MODEL), 0.5 * D_MODEL ** -0.5),
        'b_mod': nrm(ks[5], (L, 6 * D_MODEL), 0.02),
        'w_in': nrm(ks[6], (L, D_MODEL, PROJ_TOTAL), D_MODEL ** -0.5),
        'hy_conv_w': nrm(ks[7], (L, HY_SHORT, PROJ_HY), HY_SHORT ** -0.5),
        'hy_conv_b': nrm(ks[8], (L, PROJ_HY), 0.02),
        'hy_f_w1': nrm(ks[9], (L, HY_POS_DIM, HY_FILTER_HIDDEN), HY_POS_DIM ** -0.5),
        'hy_f_b1': nrm(ks[10], (L, HY_FILTER_HIDDEN), 0.02),
        'hy_f_freq1': 1.0 + nrm(ks[11], (L, HY_FILTER_HIDDEN), 0.02),
        'hy_f_w2': nrm(ks[12], (L, HY_FILTER_HIDDEN, HY_FILTER_HIDDEN), HY_FILTER_HIDDEN ** -0.5),
        'hy_f_b2': nrm(ks[13], (L, HY_FILTER_HIDDEN), 0.02),
        'hy_f_freq2': 1.0 + nrm(ks[14], (L, HY_FILTER_HIDDEN), 0.02),
        'hy_f_w3': nrm(ks[15], (L, HY_FILTER_HIDDEN, 2 * HY_ORDER * HY_WIDTH), HY_FILTER_SCALE * HY_FILTER_HIDDEN ** -0.5),
        'hy_f_b3': nrm(ks[16], (L, 2 * HY_ORDER * HY_WIDTH), 0.002),
        'hy_skip': nrm(ks[17], (L, HY_ORDER, HY_WIDTH), 0.1),
        'attn_sink': nrm(ks[18], (L, ATT_HEADS), 0.5),
        'w_out': nrm(ks[19], (L, D_MIX, D_MODEL), DEEPNORM_BETA * D_MIX ** -0.5),
        'ln1_g': 1.0 + nrm(ks[20], (L, D_MODEL), 0.02),
        'ln1_b': nrm(ks[21], (L, D_MODEL), 0.02),
        'peer_wq': nrm(ks[22], (L, D_MODEL, PEER_HEADS * PEER_QDIM), D_MODEL ** -0.5),
        'peer_keys1': nrm(ks[23], (L, PEER_KEYS, PEER_QDIM // 2), (PEER_QDIM // 2) ** -0.5),
        'peer_keys2': nrm(ks[24], (L, PEER_KEYS, PEER_QDIM // 2), (PEER_QDIM // 2) ** -0.5),
        'peer_u': nrm(ks[25], (L, PEER_EXPERTS, D_MODEL), D_MODEL ** -0.5),
        'peer_v': nrm(ks[26], (L, PEER_EXPERTS, D_MODEL), DEEPNORM_BETA),
        'ln2_g': 1.0 + nrm(ks[27], (L, D_MODEL), 0.02),
        'ln2_b': nrm(ks[28], (L, D_MODEL), 0.02),
    }


def reference(x, c, ctx, c_ctx, w_mod, b_mod, w_in, hy_conv_w, hy_conv_b, hy_f_w1, hy_f_b1,
              hy_f_freq1, hy_f_w2, hy_f_b2, hy_f_freq2, hy_f_w3, hy_f_b3, hy_skip, attn_sink,
              w_out, ln1_g, ln1_b, peer_wq, peer_keys1, peer_keys2, peer_u, peer_v, ln2_g, ln2_b):
    b, n, d = x.shape
    n_ctx = ctx.shape[1]
    cos, sin = axial_rope_tables(n)
    for l in range(DEPTH):
        last = l == DEPTH - 1
        mod = (jax.nn.silu(c) @ w_mod[l] + b_mod[l]).reshape(b, 6, 1, d)
        sh1, sc1, g1, sh2, sc2, g2 = (mod[:, i] for i in range(6))
        mod_c = (jax.nn.silu(c_ctx) @ w_mod[l] + b_mod[l]).reshape(6, d)
        csh1, csc1, cg1, csh2, csc2, cg2 = (mod_c[i] for i in range(6))
        hy_params = (hy_conv_w[l], hy_conv_b[l], hy_f_w1[l], hy_f_b1[l], hy_f_freq1[l], hy_f_w2[l],
                     hy_f_b2[l], hy_f_freq2[l], hy_f_w3[l], hy_f_b3[l], hy_skip[l])
        peer_params = (peer_wq[l], peer_keys1[l], peer_keys2[l], peer_u[l], peer_v[l])

        h_lat = x * (1.0 + sc1) + sh1
        h_ctx = ctx * (1.0 + csc1) + csh1
        p_lat = h_lat @ w_in[l]
        kv_ctx = h_ctx @ w_in[l][:, KV_START:]
        k_c = kv_ctx[..., :PROJ_KV].reshape(b, n_ctx, ATT_KV_HEADS, HEAD_DIM)
        v_c = kv_ctx[..., PROJ_KV:].reshape(b, n_ctx, ATT_KV_HEADS, HEAD_DIM)
        q_lat = apply_axial_rope(p_lat[..., PROJ_HY:KV_START].reshape(b, n, ATT_HEADS, HEAD_DIM), cos, sin)
        k_lat = apply_axial_rope(p_lat[..., KV_START:KV_START + PROJ_KV].reshape(b, n, ATT_KV_HEADS, HEAD_DIM), cos, sin)
        v_lat = p_lat[..., KV_START + PROJ_KV:].reshape(b, n, ATT_KV_HEADS, HEAD_DIM)
        att_lat = latent_attention(q_lat, k_lat, v_lat, k_c, v_c, attn_sink[l])
        hy_lat = hyena_mixer(p_lat[..., :PROJ_HY], *hy_params)
        y_lat = jnp.concatenate([hy_lat, att_lat], axis=-1) @ w_out[l]
        if not last:
            p_c = h_ctx @ w_in[l][:, :KV_START]
            q_c = p_c[..., PROJ_HY:].reshape(b, n_ctx, ATT_HEADS, HEAD_DIM)
            att_c = context_attention(q_c, k_c, v_c, attn_sink[l])
            hy_c = hyena_mixer(p_c[..., :PROJ_HY], *hy_params)
            y_c = jnp.concatenate([hy_c, att_c], axis=-1) @ w_out[l]
            ctx = layer_norm(DEEPNORM_ALPHA * ctx + cg1 * y_c, ln1_g[l], ln1_b[l])
            ctx = layer_norm(DEEPNORM_ALPHA * ctx + cg2 * peer_mixer(ctx * (1.0 + csc2) + csh2, *peer_params),
                             ln2_g[l], ln2_b[l])
        x = layer_norm(DEEPNORM_ALPHA * x + g1 * y_lat, ln1_g[l], ln1_b[l])
        x = layer_norm(DEEPNORM_ALPHA * x + g2 * peer_mixer(x * (1.0 + sc2) + sh2, *peer_params),
                       ln2_g[l], ln2_b[l])
    return x
```

```python
import math
from contextlib import ExitStack

import numpy as np
import ml_dtypes
import concourse.bass as bass
import concourse.mybir as mybir
from concourse.bass_utils import run_bass_kernel_spmd

F32 = mybir.dt.float32
BF16 = mybir.dt.bfloat16
AF = mybir.ActivationFunctionType
ALU = mybir.AluOpType
AX = mybir.AxisListType

N_CORES = 8
SEQ = 4096
DM = 1024
OWN = 2048
EXT = 2304
NFFT = 8192
LN_EPS = 1e-5
ALPHA = 2.0 ** 0.25
MAGIC = 12582912.0


class Buf:
    __slots__ = ("w", "r")

    def __init__(self):
        self.w = None
        self.r = {}


class Phase(ExitStack):
    def __init__(self, trk):
        super().__init__()
        self._trk = trk

    def __exit__(self, *a):
        if a[0] is None:
            self._trk.barrier()
        return super().__exit__(*a)


def bufs(n):
    return [Buf() for _ in range(n)]


class Trk:
    def __init__(self, nc, stack, n_dma_sems=32):
        self.nc = nc
        self.eng = {"pe": nc.tensor, "act": nc.scalar, "dve": nc.vector, "pool": nc.gpsimd, "sp": nc.sync}
        self.sems = {}
        self.cnt = {}
        for k in ["pe", "act", "dve", "pool"]:
            self.sems[k] = stack.enter_context(nc.semaphore("s_" + k))
            self.cnt[k] = 0
        self.dsems = []
        for i in range(n_dma_sems):
            key = "d%d" % i
            self.sems[key] = stack.enter_context(nc.semaphore("s_" + key))
            self.cnt[key] = 0
            self.dsems.append(key)
        self.dnext = 0
        self.waited = {k: {} for k in self.eng}
        self.n_inst = 0

    def _wait(self, e, key, val):
        if self.waited[e].get(key, 0) >= val:
            return
        self.eng[e].wait_ge(self.sems[key], val)
        self.waited[e][key] = val
        self.n_inst += 1

    def _deps(self, e, reads, writes):
        deps = {}
        for b in reads:
            if b.w is not None and deps.get(b.w[0], 0) < b.w[1]:
                deps[b.w[0]] = b.w[1]
        for b in writes:
            if b.w is not None and deps.get(b.w[0], 0) < b.w[1]:
                deps[b.w[0]] = b.w[1]
            for k, v in b.r.items():
                if deps.get(k, 0) < v:
                    deps[k] = v
        for k, v in deps.items():
            if k == e and e == "pe":
                continue
            self._wait(e, k, v)

    def _mark(self, tok, reads, writes):
        k, v = tok
        for b in writes:
            b.w = tok
            b.r = {}
        for b in reads:
            if b.r.get(k, 0) < v:
                b.r[k] = v

    def op(self, e, fn, reads=(), writes=()):
        self._deps(e, reads, writes)
        inst = fn(self.eng[e])
        self.cnt[e] += 1
        inst.then_inc(self.sems[e], 1)
        self._mark((e, self.cnt[e]), reads, writes)
        self.n_inst += 1
        return inst

    def dma(self, out, in_, reads=(), writes=(), q="sp"):
        key = self.dsems[self.dnext]
        self.dnext = (self.dnext + 1) % len(self.dsems)
        if self.cnt[key] > 0:
            self._wait(q, key, self.cnt[key])
        self._deps(q, reads, writes)
        inst = self.eng[q].dma_start(out=out, in_=in_)
        self.cnt[key] += 16
        inst.then_inc(self.sems[key], 16)
        self._mark((key, self.cnt[key]), reads, writes)
        self.n_inst += 1
        return inst

    def barrier(self):
        for e in ["pe", "act", "dve", "pool", "sp"]:
            for k in ["pe", "act", "dve", "pool"] + self.dsems:
                if self.cnt[k] > 0 and not (k == e and e == "pe"):
                    self._wait(e, k, self.cnt[k])

    def drain(self, q="sp"):
        for k in self.dsems:
            if self.cnt[k] > 0:
                self._wait(q, k, self.cnt[k])
        for k in ["pe", "act", "dve", "pool"]:
            if self.cnt[k] > 0:
                self._wait(q, k, self.cnt[k])


INPUT_SPECS = {}


def build(stage="full", dbg=()):
    nc = bass.Bass("TRN2", target_bir_lowering=False)
    D = {}

    def din(name, shape, dt=F32):
        D[name] = nc.dram_tensor(name, list(shape), dt, kind="ExternalInput").ap()
        INPUT_SPECS[name] = (tuple(shape), dt)

    def dscr(name, shape, dt=F32):
        D[name] = nc.dram_tensor(name, list(shape), dt).ap()

    def dout(name, shape, dt=F32):
        D[name] = nc.dram_tensor(name, list(shape), dt, kind="ExternalOutput").ap()

    din("xT_full", [DM, SEQ]); din("xT_ext", [DM, EXT]); din("x_own", [OWN, DM]); din("ctxT", [DM, 256])
    din("cT", [128, 8]); din("cctxT", [128, 8])
    din("w_mod", [DM, 6144]); din("b_modT", [128, 48]); din("b_mod_row", [1, 6144])
    din("w_in", [DM, 2304]); din("w_in_perm", [DM, 640])
    din("conv_w", [3, 1536]); din("conv_b", [1, 1536]); din("conv_bT", [128, 12])
    din("f_w1", [17, 64]); din("f_b1", [64, 1]); din("f_f1", [64, 1])
    din("f_w2", [64, 64]); din("f_b2", [64, 1]); din("f_f2", [64, 1])
    din("f_w3", [64, 2048]); din("f_b3", [1, 2048]); din("hy_skip", [2, 512])
    din("zT", [17, SEQ]); din("window", [SEQ, 512])
    din("dft_c", [32, 128, 4096], BF16); din("dft_s", [32, 128, 4096], BF16)
    din("nyq_fwd", [128, 4096], BF16); din("altrow", [1, 4096], BF16)
    din("dfto_c", [4, 128, 16384], BF16); din("dfto_s", [4, 128, 16384], BF16)
    din("rope_c", [64, EXT]); din("rope_s", [64, EXT])
    din("mask_prev", [128, 16 * 128], BF16); din("mask_next", [128, 16 * 128], BF16); din("validT", [128, 256])
    din("sink_row", [1, 8])
    din("w_out", [DM, DM]); din("ln1_g", [1, DM]); din("ln1_b", [1, DM]); din("ln2_g", [1, DM]); din("ln2_b", [1, DM])
    din("peer_wq", [DM, 2048]); din("keys1T", [128, 128]); din("keys2T", [128, 128])
    din("peer_uT", [DM, 16384]); din("peer_v", [16384, DM])
    dout("out", [OWN, DM])
    for nm, shp, dt_ in dbg:
        dout(nm, shp, dt_)
    dbgn = {nm for nm, _, _ in dbg}
    dscr("s_x2T", [512, OWN], BF16); dscr("s_attT", [64, 8 * OWN], BF16)
    dscr("s_spec", [2, 32, 128, 1024]); dscr("s_x1", [OWN, DM]); dscr("s_hyT", [512, OWN], BF16); dscr("s_qT", [128, 16 * OWN], BF16)

    SB = {"s_x2T": bufs(16), "s_attT": bufs(32), "s_spec": bufs(64), "s_x1": bufs(16), "s_hyT": bufs(16), "s_qT": bufs(64)}
    with ExitStack() as st:
        T = Trk(nc, st)
        PSB = [st.enter_context(nc.psum_tensor("ps%d" % i, [128, 512], F32)) for i in range(8)]
        PSBUF = bufs(8)
        psn = [0]

        def psum():
            i = psn[0] % 8
            psn[0] += 1
            return PSB[i], PSBUF[i]

        uniq = [0]

        def sbt(stack, name, shape, dt=F32):
            uniq[0] += 1
            return stack.enter_context(nc.sbuf_tensor("%s_%d" % (name, uniq[0]), list(shape), dt))

        def mm(ps, lhsT, rhs, start, stop, reads, writes):
            T.op("pe", lambda e: e.matmul(ps, lhsT=lhsT, rhs=rhs, start=start, stop=stop), reads=reads, writes=writes)

        evn = [0]

        def evac(out, in_, reads, writes, func=None, scale=None, bias=None):
            if func is not None or scale is not None or bias is not None:
                kw = {}
                if scale is not None:
                    kw["scale"] = scale
                if bias is not None:
                    kw["bias"] = bias
                T.op("act", lambda e: e.activation(out=out, in_=in_, func=func or AF.Identity, **kw), reads=reads, writes=writes)
                return
            evn[0] += 1
            if evn[0] % 2:
                T.op("act", lambda e: e.copy(out=out, in_=in_), reads=reads, writes=writes)
            else:
                T.op("dve", lambda e: e.tensor_copy(out=out, in_=in_), reads=reads, writes=writes)

        ident = sbt(st, "ident", [128, 128]); b_ident = Buf()
        identb = sbt(st, "identb", [128, 128], BF16); b_identb = Buf()
        onesb = sbt(st, "onesb", [128, 128], BF16); b_onesb = Buf()
        modT = sbt(st, "modT", [128, 48, 2]); b_modT = Buf()
        ops1 = sbt(st, "ops1", [128, 16, 2]); b_ops1 = Buf()
        g1b = sbt(st, "g1b", [128, DM]); b_g1b = Buf()
        g2b = sbt(st, "g2b", [128, DM]); b_g2b = Buf()
        epsc = sbt(st, "epsc", [128, 1]); b_epsc = Buf()
        T.op("pool", lambda e: e.memset(ident[:], 1.0), writes=[b_ident])
        T.op("pool", lambda e: e.affine_select(out=ident[:], in_=ident[:], pattern=[[-1, 128]], compare_op=ALU.is_equal,
                                               fill=0.0, base=0, channel_multiplier=1), reads=[b_ident], writes=[b_ident])
        T.op("dve", lambda e: e.tensor_copy(out=identb[:], in_=ident[:]), reads=[b_ident], writes=[b_identb])
        T.op("pool", lambda e: e.memset(onesb[:], 1.0), writes=[b_onesb])
        T.op("pool", lambda e: e.memset(epsc[:], LN_EPS), writes=[b_epsc])

        with Phase(T) as s0:
            sc = sbt(s0, "sc", [128, 2, 8]); b_sc = Buf()
            screp = sbt(s0, "screp", [128, 8, 128]); b_screp = Buf()
            bmT = sbt(s0, "bmT", [128, 48]); b_bmT = Buf()
            wst = [sbt(s0, "wmst%d" % i, [128, 8, 512]) for i in range(2)]; b_wst = bufs(2)
            T.dma(sc[:, 0, :], D["cT"][:, :], writes=[b_sc])
            T.dma(sc[:, 1, :], D["cctxT"][:, :], writes=[b_sc])
            T.dma(bmT[:], D["b_modT"][:, :], writes=[b_bmT])
            T.op("act", lambda e: e.activation(out=sc[:], in_=sc[:], func=AF.Silu), reads=[b_sc], writes=[b_sc])
            for ch in range(8):
                T.op("dve", lambda e, ch=ch: e.tensor_scalar(out=screp[:, ch, :], in0=ident[:], scalar1=0.0, scalar2=sc[:, 0, ch:ch + 1],
                                                              op0=ALU.mult, op1=ALU.add), reads=[b_ident, b_sc], writes=[b_screp])
            T.dma(g1b[:], D["b_mod_row"][0:1, 2048:3072].partition_broadcast(128), writes=[b_g1b])
            T.dma(g2b[:], D["b_mod_row"][0:1, 5120:6144].partition_broadcast(128), writes=[b_g2b])
            wm = D["w_mod"].rearrange("(ch p) n -> p ch n", p=128)
            psm, b_psm = psum()
            for blk in range(12):
                w = wst[blk % 2]; bw = b_wst[blk % 2]
                T.dma(w[:], wm[:, :, blk * 512:(blk + 1) * 512], writes=[bw])
                for j in range(4):
                    col = (blk * 4 + j) * 2
                    for ch in range(8):
                        mm(psm[:, col:col + 2], w[:, ch, j * 128:(j + 1) * 128], sc[:, :, ch], ch == 0, ch == 7,
                           [bw, b_sc], [b_psm])
                if blk in (4, 5, 10, 11):
                    psg, b_psg = psum()
                    for ch in range(8):
                        mm(psg[:], screp[:, ch, :], w[:, ch, :], ch == 0, ch == 7, [bw, b_screp], [b_psg])
                    gb, bgb = (g1b, b_g1b) if blk in (4, 5) else (g2b, b_g2b)
                    off = (blk % 2) * 512
                    T.op("dve", lambda e, gb=gb, off=off, psg=psg: e.tensor_tensor(out=gb[:, off:off + 512], in0=gb[:, off:off + 512],
                                                                                   in1=psg[:], op=ALU.add), reads=[b_psg, bgb], writes=[bgb])
            T.op("dve", lambda e: e.tensor_tensor(out=modT[:], in0=psm[:, 0:96].rearrange("p (a b) -> p a b", b=2),
                                                  in1=bmT[:].unsqueeze(2).to_broadcast([128, 48, 2]), op=ALU.add),
                 reads=[b_psm, b_bmT], writes=[b_modT])
            T.op("dve", lambda e: e.tensor_scalar(out=ops1[:, 0:8, :], in0=modT[:, 8:16, :], scalar1=1.0, scalar2=None, op0=ALU.add),
                 reads=[b_modT], writes=[b_ops1])
            T.op("dve", lambda e: e.tensor_scalar(out=ops1[:, 8:16, :], in0=modT[:, 32:40, :], scalar1=1.0, scalar2=None, op0=ALU.add),
                 reads=[b_modT], writes=[b_ops1])

        def dbg_dump(name, src_ap, rb):
            if name in dbgn:
                T.dma(D[name], src_ap, reads=[rb], writes=[Buf()])

        dbg_dump("d_modT", modT[:].rearrange("p a b -> p (a b)"), b_modT)
        dbg_dump("d_g1b", g1b[:], b_g1b)

        if stage == "p0":
            T.drain()
            return nc

        with Phase(T) as sa:
            hTe = sbt(sa, "hTe", [128, 8, EXT + 2], BF16); b_hTe = bufs(8)
            hcT = sbt(sa, "hcT", [128, 8, 256], BF16); b_hcT = Buf()
            valid = sbt(sa, "valid", [128, 256]); b_valid = Buf()
            T.dma(valid[:], D["validT"][:, :], writes=[b_valid])
            with Phase(T) as s1:
                xst = [sbt(s1, "xst%d" % i, [128, EXT]) for i in range(2)]; b_xst = bufs(2)
                xte = D["xT_ext"].rearrange("(ch p) n -> p ch n", p=128)
                cte = D["ctxT"].rearrange("(ch p) n -> p ch n", p=128)
                for ch in range(8):
                    x_, bx = xst[ch % 2], b_xst[ch % 2]
                    T.dma(x_[:], xte[:, ch, :], writes=[bx])
                    evac(hTe[:, ch, 0:EXT], x_[:], [bx, b_ops1, b_modT], [b_hTe[ch]], scale=ops1[:, ch, 0:1], bias=modT[:, ch, 0:1])
                    T.op("dve", lambda e, ch=ch: e.tensor_tensor(out=hTe[:, ch, 0:128], in0=hTe[:, ch, 0:128], in1=valid[:, 0:128], op=ALU.mult),
                         reads=[b_hTe[ch], b_valid], writes=[b_hTe[ch]])
                    T.op("dve", lambda e, ch=ch: e.tensor_tensor(out=hTe[:, ch, EXT - 128:EXT], in0=hTe[:, ch, EXT - 128:EXT], in1=valid[:, 128:256], op=ALU.mult),
                         reads=[b_hTe[ch], b_valid], writes=[b_hTe[ch]])
                for ch in range(8):
                    x_, bx = xst[ch % 2], b_xst[ch % 2]
                    T.dma(x_[:, 0:256], cte[:, ch, :], writes=[bx])
                    evac(hcT[:, ch, :], x_[:, 0:256], [bx, b_ops1, b_modT], [b_hcT], scale=ops1[:, ch, 1:2], bias=modT[:, ch, 1:2])

            win_r = D["w_in"].rearrange("(ch p) n -> p ch n", p=128)
            with Phase(T) as s2:
                wsg = sbt(s2, "wsg", [128, 8, 512]); b_wsg = Buf()
                cwb = sbt(s2, "cwb", [128, 3, 512]); b_cwb = Buf()
                cbT = sbt(s2, "cbT", [128, 12]); b_cbT = Buf()
                Wk = sbt(s2, "Wk", [128, 3, 8, 512], BF16); b_Wk = Buf()
                xo = [sbt(s2, "x2o%d" % i, [128, 512], BF16) for i in range(2)]; b_xo = bufs(2)
                T.dma(wsg[:], win_r[:, :, 1024:1536], writes=[b_wsg])
                T.dma(cwb[:], D["conv_w"][:, 1024:1536].partition_broadcast(128), writes=[b_cwb])
                T.dma(cbT[:], D["conv_bT"][:, :], writes=[b_cbT])
                for k in range(3):
                    T.op("dve", lambda e, k=k: e.tensor_tensor(out=Wk[:, k, :, :], in0=wsg[:], in1=cwb[:, k:k + 1, :].to_broadcast([128, 8, 512]), op=ALU.mult),
                         reads=[b_wsg, b_cwb], writes=[b_Wk])
                n_it = 0
                for cc in range(4):
                    for tb in range(4):
                        ps, bps = psum()
                        first = True
                        for k in range(3):
                            for ch in range(8):
                                c0 = tb * 512 + 127 + k
                                mm(ps[:], Wk[:, k, ch, cc * 128:(cc + 1) * 128], hTe[:, ch, c0:c0 + 512], first, (k == 2 and ch == 7),
                                   [b_Wk, b_hTe[ch]], [bps])
                                first = False
                        o_, bo = xo[n_it % 2], b_xo[n_it % 2]
                        n_it += 1
                        evac(o_[:], ps[:], [bps, b_cbT], [bo], bias=cbT[:, 8 + cc:9 + cc])
                        T.dma(D["s_x2T"][cc * 128:(cc + 1) * 128, tb * 512:(tb + 1) * 512], o_[:], reads=[bo], writes=[SB["s_x2T"][cc * 4 + tb]])

            qT = sbt(sa, "qT", [64, 8, OWN], BF16); b_qT = bufs(8)
            kT = sbt(sa, "kT", [64, 2, EXT], BF16); b_kT = bufs(2)
            kcT = sbt(sa, "kcT", [64, 2, 256], BF16); b_kcT = Buf()
            vaug = sbt(sa, "vaug", [128, 20, 2, 64], BF16); b_vaug = bufs(20)
            with Phase(T) as s3:
                wst3 = sbt(s3, "wst3", [128, 8, 768]); b_wst3 = Buf()
                wqk = sbt(s3, "wqk", [128, 8, 768], BF16); b_wqk = Buf()
                wqkp = sbt(s3, "wqkp", [128, 8, 640], BF16); b_wqkp = Buf()
                rc = sbt(s3, "rc", [64, EXT]); b_rc = Buf()
                rs = sbt(s3, "rs", [64, EXT]); b_rs = Buf()
                t1 = [sbt(s3, "rt1_%d" % i, [64, 512]) for i in range(2)]; b_t1 = bufs(2)
                t2 = [sbt(s3, "rt2_%d" % i, [64, 512]) for i in range(2)]; b_t2 = bufs(2)
                T.dma(rc[:], D["rope_c"][:, :], writes=[b_rc])
                T.dma(rs[:], D["rope_s"][:, :], writes=[b_rs])
                T.dma(wst3[:], win_r[:, :, 1536:2304], writes=[b_wst3])
                T.op("dve", lambda e: e.tensor_copy(out=wqk[:], in_=wst3[:]), reads=[b_wst3], writes=[b_wqk])
                T.dma(wst3[:, :, 0:640], D["w_in_perm"].rearrange("(ch p) n -> p ch n", p=128), reads=[], writes=[b_wst3])
                T.op("dve", lambda e: e.tensor_copy(out=wqkp[:], in_=wst3[:, :, 0:640]), reads=[b_wst3], writes=[b_wqkp])
                n_it = 0
                for hd in range(10):
                    if hd < 8:
                        blocks = [(128 + i * 512, 512) for i in range(4)]
                    else:
                        blocks = [(i * 512, 512) for i in range(4)] + [(2048, 256)]
                    for (e0, nb) in blocks:
                        ps1, bp1 = psum()
                        ps2, bp2 = psum()
                        for ch in range(8):
                            mm(ps1[0:64, 0:nb], wqk[:, ch, hd * 64:(hd + 1) * 64], hTe[:, ch, e0:e0 + nb], ch == 0, ch == 7, [b_wqk, b_hTe[ch]], [bp1])
                        for ch in range(8):
                            mm(ps2[0:64, 0:nb], wqkp[:, ch, hd * 64:(hd + 1) * 64], hTe[:, ch, e0:e0 + nb], ch == 0, ch == 7, [b_wqkp, b_hTe[ch]], [bp2])
                        a_, ba = t1[n_it % 2], b_t1[n_it % 2]
                        c_, bc = t2[n_it % 2], b_t2[n_it % 2]
                        n_it += 1
                        T.op("dve", lambda e, a_=a_, ps1=ps1, e0=e0, nb=nb: e.tensor_tensor(out=a_[:, 0:nb], in0=ps1[0:64, 0:nb], in1=rc[:, e0:e0 + nb], op=ALU.mult),
                             reads=[bp1, b_rc], writes=[ba])
                        T.op("dve", lambda e, c_=c_, ps2=ps2, e0=e0, nb=nb: e.tensor_tensor(out=c_[:, 0:nb], in0=ps2[0:64, 0:nb], in1=rs[:, e0:e0 + nb], op=ALU.mult),
                             reads=[bp2, b_rs], writes=[bc])
                        if hd < 8:
                            dst, bd = qT[:, hd, e0 - 128:e0 - 128 + nb], b_qT[hd]
                        else:
                            dst, bd = kT[:, hd - 8, e0:e0 + nb], b_kT[hd - 8]
                        T.op("pool", lambda e, dst=dst, a_=a_, c_=c_, nb=nb: e.tensor_tensor(out=dst, in0=a_[:, 0:nb], in1=c_[:, 0:nb], op=ALU.add),
                             reads=[ba, bc], writes=[bd])
                for g in range(2):
                    ps1, bp1 = psum()
                    for ch in range(8):
                        mm(ps1[0:64, 0:256], wqk[:, ch, 512 + g * 64:512 + (g + 1) * 64], hcT[:, ch, :], ch == 0, ch == 7, [b_wqk, b_hcT], [bp1])
                    evac(kcT[:, g, :], ps1[0:64, 0:256], [bp1], [b_kcT])
                for tl in range(20):
                    ps1, bp1 = psum()
                    for ch in range(8):
                        lt = hTe[:, ch, tl * 128:(tl + 1) * 128] if tl < 18 else hcT[:, ch, (tl - 18) * 128:(tl - 17) * 128]
                        mm(ps1[:, 0:128], lt, wqk[:, ch, 640:768], ch == 0, ch == 7, [b_wqk, b_hcT] + b_hTe, [bp1])
                    evac(vaug[:, tl, :, :], ps1[:, 0:128].rearrange("p (g d) -> p g d", g=2), [bp1], [b_vaug[tl]])

            dbg_dump("d_qT", qT[:].rearrange("p a b -> p (a b)"), b_qT[7])
            dbg_dump("d_kT", kT[:].rearrange("p a b -> p (a b)"), b_kT[1])

            with Phase(T) as s4:
                mprev = sbt(s4, "mprev", [128, 16, 128], BF16); b_mprev = Buf()
                mnext = sbt(s4, "mnext", [128, 16, 128], BF16); b_mnext = Buf()
                esk = sbt(s4, "esk", [64, 8]); b_esk = Buf()
                E = [[sbt(s4, "E%d_%d" % (i, j), [128, 512], BF16) for j in range(5)] for i in range(2)]
                b_E = [bufs(5) for _ in range(2)]
                zt = [sbt(s4, "zt%d" % i, [64, 512]) for i in range(2)]; b_zt = bufs(2)
                ao = [sbt(s4, "ao%d" % i, [64, 512], BF16) for i in range(2)]; b_ao = bufs(2)
                T.dma(mprev[:], D["mask_prev"].rearrange("p (a b) -> p a b", b=128), writes=[b_mprev])
                T.dma(mnext[:], D["mask_next"].rearrange("p (a b) -> p a b", b=128), writes=[b_mnext])
                T.dma(esk[:], D["sink_row"][0:1, :].partition_broadcast(64), writes=[b_esk])
                T.op("act", lambda e: e.activation(out=esk[:], in_=esk[:], func=AF.Exp), reads=[b_esk], writes=[b_esk])
                it = 0
                for qb in range(16):
                    for g in range(2):
                        Eb, bE = E[it % 2], b_E[it % 2]
                        z_, bz = zt[it % 2], b_zt[it % 2]
                        a_, ba = ao[it % 2], b_ao[it % 2]
                        it += 1
                        rhs_q = qT[:, 4 * g:4 * g + 4, qb * 128:(qb + 1) * 128]
                        rq = b_qT[4 * g:4 * g + 4]
                        for kb in range(5):
                            if kb < 3:
                                lt, rl = kT[:, g, (qb + kb) * 128:(qb + kb + 1) * 128], [b_kT[g]]
                            else:
                                lt, rl = kcT[:, g, (kb - 3) * 128:(kb - 2) * 128], [b_kcT]
                            ps, bps = psum()
                            mm(ps[:].rearrange("p (a b) -> p a b", a=4), lt, rhs_q, True, True, rl + rq, [bps])
                            T.op("act", lambda e, ps=ps, o=Eb[kb]: e.activation(out=o[:], in_=ps[:], func=AF.Exp, scale=0.125), reads=[bps], writes=[bE[kb]])
                            if kb == 0 or kb == 2:
                                mk, bmk = (mprev, b_mprev) if kb == 0 else (mnext, b_mnext)
                                T.op("pool", lambda e, o=Eb[kb], mk=mk, qb=qb: e.tensor_tensor(
                                    out=o[:].rearrange("p (a b) -> p a b", a=4), in0=o[:].rearrange("p (a b) -> p a b", a=4),
                                    in1=mk[:, qb:qb + 1, :].to_broadcast([128, 4, 128]), op=ALU.mult), reads=[bE[kb], bmk], writes=[bE[kb]])
                        pso, bpo = psum()
                        psz, bpz = psum()
                        for kb in range(5):
                            vt = (qb + kb) if kb < 3 else (18 + kb - 3)
                            mm(pso[0:64, :], vaug[:, vt, g, :], Eb[kb][:], kb == 0, kb == 4, [b_vaug[vt], bE[kb]], [bpo])
                        for kb in range(5):
                            mm(psz[0:64, :], onesb[:, 0:64], Eb[kb][:], kb == 0, kb == 4, [b_onesb, bE[kb]], [bpz])
                        T.op("dve", lambda e, z_=z_, psz=psz, g=g: e.tensor_tensor(
                            out=z_[:].rearrange("p (a b) -> p a b", a=4), in0=psz[0:64, :].rearrange("p (a b) -> p a b", a=4),
                            in1=esk[:, 4 * g:4 * g + 4].unsqueeze(2).to_broadcast([64, 4, 128]), op=ALU.add), reads=[bpz, b_esk], writes=[bz])
                        T.op("dve", lambda e, z_=z_: e.reciprocal(out=z_[:], in_=z_[:]), reads=[bz], writes=[bz])
                        T.op("dve", lambda e, a_=a_, pso=pso, z_=z_: e.tensor_tensor(out=a_[:], in0=pso[0:64, :], in1=z_[:], op=ALU.mult),
                             reads=[bpo, bz], writes=[ba])
                        dst = D["s_attT"].rearrange("p (h n) -> p h n", h=8)[:, 4 * g:4 * g + 4, qb * 128:(qb + 1) * 128]
                        T.dma(dst, a_[:].rearrange("p (a b) -> p a b", a=4), reads=[ba], writes=[SB["s_attT"][qb * 2 + g]])

        if "d_attT" in dbgn:
            T.dma(D["d_attT"], D["s_attT"], reads=SB["s_attT"], writes=[Buf()])
        if "d_x2T" in dbgn:
            T.dma(D["d_x2T"], D["s_x2T"], reads=SB["s_x2T"], writes=[Buf()])
        if stage == "pa":
            T.drain()
            return nc

        with Phase(T) as sbk:
            v_tm = sbt(sbk, "v_tm", [128, 32, 512], BF16); b_v = bufs(32)
            x1_tm = sbt(sbk, "x1_tm", [128, 32, 512], BF16); b_x1 = bufs(32)
            h2f = sbt(sbk, "h2f", [64, SEQ], BF16); b_h2f = Buf()
            with Phase(T) as s1:
                hTf = sbt(s1, "hTf", [128, 8, SEQ + 2], BF16); b_hTf2 = [bufs(2) for _ in range(8)]
                T.op("pool", lambda e: e.memset(hTf[:, :, 0:1], 0.0), writes=[b_hTf2[ch][0] for ch in range(8)])
                T.op("pool", lambda e: e.memset(hTf[:, :, SEQ + 1:SEQ + 2], 0.0), writes=[b_hTf2[ch][1] for ch in range(8)])

                def hTf_deps(ch, tl):
                    if tl < 15:
                        return [b_hTf2[ch][0]]
                    if tl > 16:
                        return [b_hTf2[ch][1]]
                    return b_hTf2[ch]
                with Phase(T) as s1a:
                    xst = [sbt(s1a, "xsf%d" % i, [128, 2048]) for i in range(2)]; b_xst = bufs(2)
                    xtf = D["xT_full"].rearrange("(ch p) n -> p ch n", p=128)
                    for hf in range(2):
                        for ch in range(8):
                            x_, bx = xst[ch % 2], b_xst[ch % 2]
                            T.dma(x_[:], xtf[:, ch, hf * 2048:(hf + 1) * 2048], writes=[bx])
                            evac(hTf[:, ch, 1 + hf * 2048:1 + (hf + 1) * 2048], x_[:], [bx, b_ops1, b_modT], [b_hTf2[ch][hf]],
                                 scale=ops1[:, ch, 0:1], bias=modT[:, ch, 0:1])
                with Phase(T) as s1b:
                    wsg = sbt(s1b, "wsgf", [128, 8, 512]); b_wsg = Buf()
                    cwb = sbt(s1b, "cwbf", [128, 3, 512]); b_cwb = Buf()
                    cbr = sbt(s1b, "cbr", [1, 512]); b_cbr = Buf()
                    cbb = sbt(s1b, "cbb", [1, 512], BF16); b_cbb = Buf()
                    Wk = sbt(s1b, "Wkf", [128, 3, 8, 512], BF16); b_Wk = Buf()
                    for cb in range(2):
                        dst, bdst = (v_tm, b_v) if cb == 0 else (x1_tm, b_x1)
                        T.dma(wsg[:], win_r[:, :, cb * 512:(cb + 1) * 512], writes=[b_wsg])
                        T.dma(cwb[:], D["conv_w"][:, cb * 512:(cb + 1) * 512].partition_broadcast(128), writes=[b_cwb])
                        T.dma(cbr[:], D["conv_b"][0:1, cb * 512:(cb + 1) * 512], writes=[b_cbr])
                        T.op("dve", lambda e: e.tensor_copy(out=cbb[:], in_=cbr[:]), reads=[b_cbr], writes=[b_cbb])
                        for k in range(3):
                            T.op("dve", lambda e, k=k: e.tensor_tensor(out=Wk[:, k, :, :], in0=wsg[:], in1=cwb[:, k:k + 1, :].to_broadcast([128, 8, 512]), op=ALU.mult),
                                 reads=[b_wsg, b_cwb], writes=[b_Wk])
                        for tl in range(32):
                            ps, bps = psum()
                            first = True
                            for k in range(3):
                                for ch in range(8):
                                    mm(ps[:], hTf[:, ch, tl * 128 + k:tl * 128 + k + 128], Wk[:, k, ch, :], first, False, [b_Wk] + hTf_deps(ch, tl), [bps])
                                    first = False
                            mm(ps[:], onesb[0:1, 0:128], cbb[0:1, :], False, True, [b_onesb, b_cbb], [bps])
                            evac(dst[:, tl, :], ps[:], [bps], [bdst[tl]])

            if "d_v" in dbgn:
                T.dma(D["d_v"].rearrange("(t p) c -> p t c", p=128), v_tm[:], reads=b_v, writes=[Buf()])
                T.dma(D["d_x1"].rearrange("(t p) c -> p t c", p=128), x1_tm[:], reads=b_x1, writes=[Buf()])

            with Phase(T) as s2:
                zT = sbt(s2, "zT_sb", [17, SEQ]); b_zT = Buf()
                w1 = sbt(s2, "fw1", [17, 64]); w2 = sbt(s2, "fw2", [64, 64]); b_w12 = Buf()
                pr = sbt(s2, "fpr", [64, 6]); b_pr = Buf()
                h1f = sbt(s2, "h1f", [64, SEQ]); b_h1f = Buf()
                ar = [sbt(s2, "far%d" % i, [64, 512]) for i in range(2)]; b_ar = bufs(2)
                kr = [sbt(s2, "fkr%d" % i, [64, 512]) for i in range(2)]; b_kr = bufs(2)
                T.dma(zT[:], D["zT"][:, :], writes=[b_zT])
                T.dma(w1[:], D["f_w1"][:, :], writes=[b_w12])
                T.dma(w2[:], D["f_w2"][:, :], writes=[b_w12])
                for i, nm in enumerate(["f_b1", "f_f1", "f_b2", "f_f2"]):
                    T.dma(pr[:, i:i + 1], D[nm][:, :], writes=[b_pr])
                T.op("dve", lambda e: e.tensor_tensor(out=pr[:, 4:5], in0=pr[:, 0:1], in1=pr[:, 1:2], op=ALU.mult), reads=[b_pr], writes=[b_pr])
                T.op("dve", lambda e: e.tensor_tensor(out=pr[:, 5:6], in0=pr[:, 2:3], in1=pr[:, 3:4], op=ALU.mult), reads=[b_pr], writes=[b_pr])
                it = 0
                for layer in range(2):
                    for blk in range(8):
                        ps, bps = psum()
                        if layer == 0:
                            mm(ps[0:64, :], w1[:], zT[:, blk * 512:(blk + 1) * 512], True, True, [b_w12, b_zT], [bps])
                            fcol, fbcol = pr[:, 1:2], pr[:, 4:5]
                        else:
                            mm(ps[0:64, :], w2[:], h1f[:, blk * 512:(blk + 1) * 512], True, True, [b_w12, b_h1f], [bps])
                            fcol, fbcol = pr[:, 3:4], pr[:, 5:6]
                        a_, ba = ar[it % 2], b_ar[it % 2]
                        k_, bk = kr[it % 2], b_kr[it % 2]
                        it += 1
                        T.op("dve", lambda e, a_=a_, ps=ps, fcol=fcol, fbcol=fbcol: e.tensor_scalar(out=a_[:], in0=ps[0:64, :], scalar1=fcol, scalar2=fbcol, op0=ALU.mult, op1=ALU.add),
                             reads=[bps, b_pr], writes=[ba])
                        T.op("dve", lambda e, a_=a_, k_=k_: e.tensor_scalar(out=k_[:], in0=a_[:], scalar1=1.0 / (2 * math.pi), scalar2=MAGIC, op0=ALU.mult, op1=ALU.add),
                             reads=[ba], writes=[bk])
                        T.op("dve", lambda e, k_=k_: e.tensor_scalar(out=k_[:], in0=k_[:], scalar1=MAGIC, scalar2=-2 * math.pi, op0=ALU.subtract, op1=ALU.mult),
                             reads=[bk], writes=[bk])
                        T.op("dve", lambda e, a_=a_, k_=k_: e.tensor_tensor(out=a_[:], in0=a_[:], in1=k_[:], op=ALU.add), reads=[ba, bk], writes=[ba])
                        if layer == 0:
                            T.op("act", lambda e, a_=a_, blk=blk: e.activation(out=h1f[:, blk * 512:(blk + 1) * 512], in_=a_[:], func=AF.Sin), reads=[ba], writes=[b_h1f])
                        else:
                            T.op("act", lambda e, a_=a_, blk=blk: e.activation(out=h2f[:, blk * 512:(blk + 1) * 512], in_=a_[:], func=AF.Sin), reads=[ba], writes=[b_h2f])

            tabs_c = [sbt(sbk, "tabc%d" % i, [128, 32, 128], BF16) for i in range(2)]; b_tc = bufs(2)
            tabs_s = [sbt(sbk, "tabs%d" % i, [128, 32, 128], BF16) for i in range(2)]; b_ts = bufs(2)
            dc_r = D["dft_c"].rearrange("o p (k j) -> o p k j", j=128)
            ds_r = D["dft_s"].rearrange("o p (k j) -> o p k j", j=128)
            tabn = [0]

            def load_tabs(oc):
                i = tabn[0] % 2
                tabn[0] += 1
                T.dma(tabs_c[i][:], dc_r[oc], writes=[b_tc[i]])
                T.dma(tabs_s[i][:], ds_r[oc], writes=[b_ts[i]])
                return tabs_c[i], b_tc[i], tabs_s[i], b_ts[i]

            nyr = sbt(sbk, "nyr", [1, 512]); b_nyr = Buf()
            altr = sbt(sbk, "altr", [1, 512], BF16); b_altr = Buf()
            T.dma(altr[:], D["altrow"][:, 0:512], writes=[b_altr])

            def nyq_row(src, b_src):
                i = tabn[0] % 2
                tabn[0] += 1
                T.dma(tabs_s[i][:], D["nyq_fwd"].rearrange("p (k j) -> p k j", j=128), writes=[b_ts[i]])
                psx, bpx = psum()
                for kc in range(32):
                    mm(psx[:], tabs_s[i][:, kc, :], src[:, kc, :], kc == 0, kc == 31, [b_ts[i], b_src[kc]], [bpx])
                T.op("act", lambda e: e.copy(out=nyr[:], in_=psx[0:1, :]), reads=[bpx], writes=[b_nyr])

            spec_r = D["s_spec"]
            with Phase(T) as s3:
                w3b = sbt(s3, "w3b", [64, 2048], BF16); b_w3b = Buf()
                b3b = sbt(s3, "b3b", [1, 2048], BF16); b_b3b = Buf()
                skp = sbt(s3, "skp", [1, 2, 512]); b_skp = Buf()
                with Phase(T) as s3a:
                    w3s = sbt(s3a, "w3s", [64, 2048]); b_w3s = Buf()
                    b3s = sbt(s3a, "b3s", [1, 2048]); b_b3s = Buf()
                    T.dma(w3s[:], D["f_w3"][:, :], writes=[b_w3s])
                    T.dma(b3s[:], D["f_b3"][:, :], writes=[b_b3s])
                    T.op("dve", lambda e: e.tensor_tensor(out=w3b[:, 0:1024], in0=w3s[:, 0:1024], in1=w3s[:, 1024:2048], op=ALU.add), reads=[b_w3s], writes=[b_w3b])
                    T.op("dve", lambda e: e.tensor_tensor(out=w3b[:, 1024:2048], in0=w3s[:, 0:1024], in1=w3s[:, 1024:2048], op=ALU.subtract), reads=[b_w3s], writes=[b_w3b])
                    T.op("dve", lambda e: e.tensor_tensor(out=b3b[:, 0:1024], in0=b3s[:, 0:1024], in1=b3s[:, 1024:2048], op=ALU.add), reads=[b_b3s], writes=[b_b3b])
                    T.op("dve", lambda e: e.tensor_tensor(out=b3b[:, 1024:2048], in0=b3s[:, 0:1024], in1=b3s[:, 1024:2048], op=ALU.subtract), reads=[b_b3s], writes=[b_b3b])
                hs = sbt(s3, "hs", [128, 32, 512], BF16); b_hs = bufs(32)
                hd = sbt(s3, "hd", [128, 32, 512], BF16); b_hd = bufs(32)
                wn = [sbt(s3, "wn%d" % i, [128, 512]) for i in range(2)]; b_wn = bufs(2)
                tA = [sbt(s3, "tA%d" % i, [128, 512]) for i in range(2)]; b_tA = bufs(2)
                tB = [sbt(s3, "tB%d" % i, [128, 512]) for i in range(1)]; b_tB = bufs(1)
                spt = [sbt(s3, "spt%d" % i, [128, 2, 512]) for i in range(1)] * 2; b_spt = bufs(1) * 2
                T.dma(skp[:], D["hy_skip"].rearrange("(a o) c -> a o c", a=1), writes=[b_skp])
                for o in range(2):
                    for tc in range(32):
                        w_, bw = wn[tc % 2], b_wn[tc % 2]
                        T.dma(w_[:], D["window"][tc * 128:(tc + 1) * 128, :], writes=[bw])
                        psf, bpf = psum()
                        psb, bpb = psum()
                        cf, cb_ = o * 512, 1024 + o * 512
                        mm(psf[:], h2f[:, tc * 128:(tc + 1) * 128], w3b[:, cf:cf + 512], True, False, [b_h2f, b_w3b], [bpf])
                        mm(psf[:], onesb[0:1, 0:128], b3b[0:1, cf:cf + 512], False, True, [b_onesb, b_b3b], [bpf])
                        mm(psb[:], h2f[:, tc * 128:(tc + 1) * 128], w3b[:, cb_:cb_ + 512], True, False, [b_h2f, b_w3b], [bpb])
                        mm(psb[:], onesb[0:1, 0:128], b3b[0:1, cb_:cb_ + 512], False, True, [b_onesb, b_b3b], [bpb])
                        if tc > 0:
                            T.op("dve", lambda e, psf=psf, w_=w_, tc=tc: e.tensor_tensor(out=hs[:, tc, :], in0=psf[:], in1=w_[:], op=ALU.mult), reads=[bpf, bw], writes=[b_hs[tc]])
                            T.op("dve", lambda e, psb=psb, w_=w_, tc=tc: e.tensor_tensor(out=hd[:, tc, :], in0=psb[:], in1=w_[:], op=ALU.mult), reads=[bpb, bw], writes=[b_hd[tc]])
                        else:
                            A_, bA = tA[0], b_tA[0]
                            B_, bB = tB[0], b_tB[0]
                            r0, br0 = tA[1], b_tA[1]
                            T.op("dve", lambda e, A_=A_, psf=psf, w_=w_: e.tensor_tensor(out=A_[:], in0=psf[:], in1=w_[:], op=ALU.mult), reads=[bpf, bw], writes=[bA])
                            T.op("dve", lambda e, B_=B_, psb=psb, w_=w_: e.tensor_tensor(out=B_[:], in0=psb[:], in1=w_[:], op=ALU.mult), reads=[bpb, bw], writes=[bB])
                            T.op("dve", lambda e, A_=A_, B_=B_, r0=r0: e.tensor_tensor(out=r0[0:1, :], in0=A_[0:1, :], in1=B_[0:1, :], op=ALU.subtract), reads=[bA, bB], writes=[br0])
                            T.op("dve", lambda e, A_=A_, r0=r0: e.scalar_tensor_tensor(out=A_[0:1, :], in0=r0[0:1, :], scalar=-0.5, in1=A_[0:1, :], op0=ALU.mult, op1=ALU.add), reads=[br0, bA], writes=[bA])
                            T.op("dve", lambda e, B_=B_, r0=r0: e.scalar_tensor_tensor(out=B_[0:1, :], in0=r0[0:1, :], scalar=0.5, in1=B_[0:1, :], op0=ALU.mult, op1=ALU.add), reads=[br0, bB], writes=[bB])
                            T.op("dve", lambda e, A_=A_, o=o: e.tensor_tensor(out=A_[0:1, :], in0=A_[0:1, :], in1=skp[0:1, o, :], op=ALU.add), reads=[bA, b_skp], writes=[bA])
                            T.op("dve", lambda e, B_=B_, o=o: e.tensor_tensor(out=B_[0:1, :], in0=B_[0:1, :], in1=skp[0:1, o, :], op=ALU.add), reads=[bB, b_skp], writes=[bB])
                            T.op("dve", lambda e, A_=A_: e.tensor_copy(out=hs[:, 0, :], in_=A_[:]), reads=[bA], writes=[b_hs[0]])
                            T.op("dve", lambda e, B_=B_: e.tensor_copy(out=hd[:, 0, :], in_=B_[:]), reads=[bB], writes=[b_hd[0]])
                    if "d_hs" in dbgn and o == 0:
                        T.dma(D["d_hs"].rearrange("(t p) c -> p t c", p=128), hs[:], reads=b_hs, writes=[Buf()])
                    nyq_row(hs, b_hs)
                    nxt = load_tabs(0)
                    for fc in range(32):
                        tcb, btc, tsb, bts = nxt
                        if fc + 1 < 32:
                            nxt = load_tabs(fc + 1)
                        psr, bpr = psum()
                        psi, bpi = psum()
                        for kc in range(32):
                            mm(psr[:], tcb[:, kc, :], hs[:, kc, :], kc == 0, kc == 31, [btc, b_hs[kc]], [bpr])
                        for kc in range(32):
                            mm(psi[:], tsb[:, kc, :], hd[:, kc, :], kc == 0, kc == 31, [bts, b_hd[kc]], [bpi])
                        sp, bsp = spt[fc % 2], b_spt[fc % 2]
                        T.op("act", lambda e, sp=sp, psr=psr: e.activation(out=sp[:, 0, :], in_=psr[:], func=AF.Copy, scale=2.0 / NFFT), reads=[bpr], writes=[bsp])
                        T.op("act", lambda e, sp=sp, psi=psi: e.activation(out=sp[:, 1, :], in_=psi[:], func=AF.Copy, scale=2.0 / NFFT), reads=[bpi], writes=[bsp])
                        if fc == 0:
                            T.op("act", lambda e, sp=sp: e.activation(out=sp[0:1, 1, :], in_=nyr[:], func=AF.Copy, scale=2.0 / NFFT), reads=[b_nyr, bsp], writes=[bsp])
                        T.dma(spec_r[o, fc].rearrange("p (a c) -> p a c", a=2), sp[:], reads=[bsp], writes=[SB["s_spec"][o * 32 + fc]])
            if "d_spec" in dbgn:
                T.dma(D["d_spec"], D["s_spec"], reads=SB["s_spec"], writes=[Buf()])

            with Phase(T) as s4:
                Y = sbt(s4, "Y", [128, 32, 2, 512], BF16); b_Y = bufs(32)
                stl = [sbt(s4, "stl%d" % i, [128, 2, 512]) for i in range(2)]; b_stl = bufs(2)
                tt = [sbt(s4, "tt%d" % i, [128, 512]) for i in range(4)]; b_tt = bufs(4)

                def forward(src, b_src, o):
                    nyq_row(src, b_src)
                    nxt = load_tabs(0)
                    for fc in range(32):
                        tcb, btc, tsb, bts = nxt
                        if fc + 1 < 32:
                            nxt = load_tabs(fc + 1)
                        S_, bS = stl[fc % 2], b_stl[fc % 2]
                        T.dma(S_[:], spec_r[o, fc].rearrange("p (a c) -> p a c", a=2), reads=[SB["s_spec"][o * 32 + fc]], writes=[bS])
                        psr, bpr = psum()
                        psi, bpi = psum()
                        for kc in range(32):
                            mm(psr[:], tcb[:, kc, :], src[:, kc, :], kc == 0, kc == 31, [btc, b_src[kc]], [bpr])
                        for kc in range(32):
                            mm(psi[:], tsb[:, kc, :], src[:, kc, :], kc == 0, kc == 31, [bts, b_src[kc]], [bpi])
                        T.op("dve", lambda e, psr=psr, S_=S_: e.tensor_tensor(out=tt[0][:], in0=psr[:], in1=S_[:, 0, :], op=ALU.mult), reads=[bpr, bS], writes=[b_tt[0]])
                        T.op("dve", lambda e, psi=psi, S_=S_: e.tensor_tensor(out=tt[1][:], in0=psi[:], in1=S_[:, 1, :], op=ALU.mult), reads=[bpi, bS], writes=[b_tt[1]])
                        T.op("dve", lambda e, psr=psr, S_=S_: e.tensor_tensor(out=tt[2][:], in0=psr[:], in1=S_[:, 1, :], op=ALU.mult), reads=[bpr, bS], writes=[b_tt[2]])
                        T.op("dve", lambda e, psi=psi, S_=S_: e.tensor_tensor(out=tt[3][:], in0=psi[:], in1=S_[:, 0, :], op=ALU.mult), reads=[bpi, bS], writes=[b_tt[3]])
                        T.op("pool", lambda e, fc=fc: e.tensor_tensor(out=Y[:, fc, 0, :], in0=tt[0][:], in1=tt[1][:], op=ALU.subtract), reads=[b_tt[0], b_tt[1]], writes=[b_Y[fc]])
                        T.op("pool", lambda e, fc=fc: e.tensor_tensor(out=Y[:, fc, 1, :], in0=tt[2][:], in1=tt[3][:], op=ALU.add), reads=[b_tt[2], b_tt[3]], writes=[b_Y[fc]])
                        if fc == 0:
                            T.op("dve", lambda e, psr=psr, S_=S_: e.scalar_tensor_tensor(out=Y[0:1, 0, 0, :], in0=psr[0:1, :], scalar=0.5, in1=S_[0:1, 0, :], op0=ALU.mult, op1=ALU.mult),
                                 reads=[bpr, bS, b_Y[0]], writes=[b_Y[0]])
                            T.op("dve", lambda e, psi=psi, S_=S_: e.scalar_tensor_tensor(out=Y[0:1, 0, 1, :], in0=nyr[:], scalar=0.5, in1=S_[0:1, 1, :], op0=ALU.mult, op1=ALU.mult),
                                 reads=[b_nyr, bS, b_Y[0]], writes=[b_Y[0]])

                forward(v_tm, b_v, 0)
                nxt = load_tabs(0)
                for oc in range(32):
                    tcb, btc, tsb, bts = nxt
                    if oc + 1 < 32:
                        nxt = load_tabs(oc + 1)
                    ps, bps = psum()
                    for kc in range(32):
                        mm(ps[:], tcb[:, kc, :], Y[:, kc, 0, :], kc == 0, False, [btc, b_Y[kc]], [bps])
                    for kc in range(32):
                        mm(ps[:], tsb[:, kc, :], Y[:, kc, 1, :], False, False, [bts, b_Y[kc]], [bps])
                    mm(ps[:], altr[0:1, 0:128], Y[0:1, 0, 1, :], False, True, [b_altr, b_Y[0]], [bps])
                    T.op("dve", lambda e, ps=ps, oc=oc: e.tensor_tensor(out=x1_tm[:, oc, :], in0=ps[:], in1=x1_tm[:, oc, :], op=ALU.mult), reads=[bps, b_x1[oc]], writes=[b_x1[oc]])
                if "d_y1" in dbgn:
                    T.dma(D["d_y1"].rearrange("(t p) c -> p t c", p=128), x1_tm[:], reads=b_x1, writes=[Buf()])
                forward(x1_tm, b_x1, 1)
                toc = v_tm[:].rearrange("p a b -> p (a b)").rearrange("p (k j) -> p k j", j=512)
                tos = x1_tm[:].rearrange("p a b -> p (a b)").rearrange("p (k j) -> p k j", j=512)
                x2l = [sbt(s4, "x2l%d" % i, [128, 512], BF16) for i in range(2)]; b_x2l = bufs(2)
                hyo = [sbt(s4, "hyo%d" % i, [128, 512], BF16) for i in range(2)]; b_hyo = bufs(2)
                it = 0
                for tb in range(4):
                    T.dma(toc, D["dfto_c"][tb].rearrange("p (k j) -> p k j", j=512), reads=[], writes=b_v)
                    T.dma(tos, D["dfto_s"][tb].rearrange("p (k j) -> p k j", j=512), reads=[], writes=b_x1)
                    for cc in range(4):
                        xl, bxl = x2l[it % 2], b_x2l[it % 2]
                        ho, bho = hyo[it % 2], b_hyo[it % 2]
                        it += 1
                        T.dma(xl[:], D["s_x2T"][cc * 128:(cc + 1) * 128, tb * 512:(tb + 1) * 512], reads=[SB["s_x2T"][cc * 4 + tb]], writes=[bxl])
                        ps, bps = psum()
                        for kc in range(32):
                            mm(ps[:], Y[:, kc, 0, cc * 128:(cc + 1) * 128], toc[:, kc, :], kc == 0, False, [b_v[0], b_Y[kc]], [bps])
                        for kc in range(32):
                            mm(ps[:], Y[:, kc, 1, cc * 128:(cc + 1) * 128], tos[:, kc, :], False, False, [b_x1[0], b_Y[kc]], [bps])
                        mm(ps[:], Y[0:1, 0, 1, cc * 128:(cc + 1) * 128], altr[0:1, 0:512], False, True, [b_altr, b_Y[0]], [bps])
                        T.op("dve", lambda e, ps=ps, xl=xl, ho=ho: e.tensor_tensor(out=ho[:], in0=ps[:], in1=xl[:], op=ALU.mult), reads=[bps, bxl], writes=[bho])
                        T.dma(D["s_hyT"][cc * 128:(cc + 1) * 128, tb * 512:(tb + 1) * 512], ho[:], reads=[bho], writes=[SB["s_hyT"][cc * 4 + tb]])
        if "d_hyT" in dbgn:
            T.dma(D["d_hyT"], D["s_hyT"], reads=SB["s_hyT"], writes=[Buf()])
        if stage == "pb":
            T.drain()
            return nc

        h2T = sbt(st, "h2T", [128, 8, OWN], BF16); b_h2T = bufs(16)
        lng = sbt(st, "lng", [128, DM]); lnb = sbt(st, "lnb", [128, DM]); b_ln = Buf()

        def layer_norm_tile(tt, btt, scr, bscr, mv, bmv):
            T.op("dve", lambda e: e.bn_stats(out=scr[:, 0, :], in_=tt[:, 0:512]), reads=[btt], writes=[bscr])
            T.op("dve", lambda e: e.bn_stats(out=scr[:, 1, :], in_=tt[:, 512:1024]), reads=[btt], writes=[bscr])
            T.op("dve", lambda e: e.bn_aggr(out=mv[:, 0:2], in_=scr[:].rearrange("p a b -> p (a b)")), reads=[bscr], writes=[bmv])
            T.op("act", lambda e: e.activation(out=mv[:, 2:3], in_=mv[:, 1:2], func=AF.Sqrt, bias=epsc[:, 0:1]), reads=[bmv, b_epsc], writes=[bmv])
            T.op("dve", lambda e: e.reciprocal(out=mv[:, 2:3], in_=mv[:, 2:3]), reads=[bmv], writes=[bmv])
            T.op("dve", lambda e: e.tensor_scalar(out=tt[:], in0=tt[:], scalar1=mv[:, 0:1], scalar2=mv[:, 2:3], op0=ALU.subtract, op1=ALU.mult),
                 reads=[btt, bmv], writes=[btt])
            T.op("pool", lambda e: e.tensor_tensor(out=tt[:], in0=tt[:], in1=lng[:], op=ALU.mult), reads=[btt, b_ln], writes=[btt])
            T.op("pool", lambda e: e.tensor_tensor(out=tt[:], in0=tt[:], in1=lnb[:], op=ALU.add), reads=[btt, b_ln], writes=[btt])

        sq_r = D["s_qT"].rearrange("p (k n) -> p k n", k=16)
        with Phase(T) as sq:
            wqs = sbt(sq, "wqs", [128, 8, 512]); b_wqs = Buf()
            wqb = sbt(sq, "wqb", [128, 8, 2048], BF16); b_wqb = bufs(4)
            wq_r = D["peer_wq"].rearrange("(ch p) n -> p ch n", p=128)
            for cb in range(4):
                T.dma(wqs[:], wq_r[:, :, cb * 512:(cb + 1) * 512], writes=[b_wqs])
                T.op("pool", lambda e, cb=cb: e.tensor_copy(out=wqb[:, :, cb * 512:(cb + 1) * 512], in_=wqs[:]), reads=[b_wqs], writes=[b_wqb[cb]])
            with Phase(T) as sc_:
                hyT = sbt(sc_, "hyT", [128, 4, OWN], BF16); b_hyT = Buf()
                attT = sbt(sc_, "attT", [64, 8, OWN], BF16); b_attT = Buf()
                wo_hy = sbt(sc_, "wo_hy", [128, 4, DM], BF16); wo_att = sbt(sc_, "wo_att", [64, 8, DM], BF16); b_wo = Buf()
                T.dma(lng[:], D["ln1_g"][0:1, :].partition_broadcast(128), writes=[b_ln])
                T.dma(lnb[:], D["ln1_b"][0:1, :].partition_broadcast(128), writes=[b_ln])
                T.dma(hyT[:], D["s_hyT"].rearrange("(c p) n -> p c n", p=128), reads=SB["s_hyT"], writes=[b_hyT])
                T.dma(attT[:], D["s_attT"].rearrange("p (h n) -> p h n", h=8), reads=SB["s_attT"], writes=[b_attT])
                with Phase(T) as sc1:
                    wos = sbt(sc1, "wos", [128, 4, DM]); b_wos = Buf()
                    T.dma(wos[:], D["w_out"][0:512, :].rearrange("(c p) d -> p c d", p=128), writes=[b_wos])
                    T.op("dve", lambda e: e.tensor_copy(out=wo_hy[:], in_=wos[:]), reads=[b_wos], writes=[b_wo])
                    wos2 = wos[:].rearrange("p a b -> p (a b)")[0:64, :].rearrange("p (h d) -> p h d", h=4)
                    for hh in range(2):
                        T.dma(wos2, D["w_out"][512 + hh * 256:512 + (hh + 1) * 256, :].rearrange("(h p) d -> p h d", p=64), writes=[b_wos])
                        T.op("dve", lambda e, hh=hh: e.tensor_copy(out=wo_att[:, hh * 4:(hh + 1) * 4, :], in_=wos2), reads=[b_wos], writes=[b_wo])
                xo = [sbt(sc_, "xo%d" % i, [128, DM]) for i in range(2)]; b_xo2 = bufs(2)
                tt = [sbt(sc_, "ttc%d" % i, [128, DM]) for i in range(2)]; b_ttc = bufs(2)
                scrs = [sbt(sc_, "lnscr%d" % i, [128, 2, 6]) for i in range(2)]; b_scrs = bufs(2)
                mvs = [sbt(sc_, "lnmv%d" % i, [128, 4]) for i in range(2)]; b_mvs = bufs(2)

                def pc_mm(tl):
                    res = []
                    for hf in range(2):
                        ps, bps = psum()
                        for cc in range(4):
                            mm(ps[:], hyT[:, cc, tl * 128:(tl + 1) * 128], wo_hy[:, cc, hf * 512:(hf + 1) * 512], cc == 0, False, [b_hyT, b_wo], [bps])
                        for h in range(8):
                            mm(ps[:], attT[:, h, tl * 128:(tl + 1) * 128], wo_att[:, h, hf * 512:(hf + 1) * 512], False, h == 7, [b_attT, b_wo], [bps])
                        res.append((ps, bps))
                    return res

                nxt_mm = pc_mm(0)
                for tl in range(16):
                    cur_mm = nxt_mm
                    x_, bx = xo[tl % 2], b_xo2[tl % 2]
                    t_, bt = tt[tl % 2], b_ttc[tl % 2]
                    T.dma(x_[:], D["x_own"][tl * 128:(tl + 1) * 128, :], writes=[bx])
                    for hf in range(2):
                        ps, bps = cur_mm[hf]
                        T.op("dve", lambda e, t_=t_, ps=ps, hf=hf: e.tensor_tensor(out=t_[:, hf * 512:(hf + 1) * 512], in0=ps[:], in1=g1b[:, hf * 512:(hf + 1) * 512], op=ALU.mult),
                             reads=[bps, b_g1b], writes=[bt])
                    T.op("dve", lambda e, t_=t_, x_=x_: e.scalar_tensor_tensor(out=t_[:], in0=x_[:], scalar=ALPHA, in1=t_[:], op0=ALU.mult, op1=ALU.add), reads=[bx, bt], writes=[bt])
                    if tl + 1 < 16:
                        nxt_mm = pc_mm(tl + 1)
                    layer_norm_tile(t_, bt, scrs[tl % 2], b_scrs[tl % 2], mvs[tl % 2], b_mvs[tl % 2])
                    T.dma(D["s_x1"][tl * 128:(tl + 1) * 128, :], t_[:], reads=[bt], writes=[SB["s_x1"][tl]])
                    for half in range(2):
                        ps, bps = psum()
                        for j in range(4):
                            ch = half * 4 + j
                            T.op("pe", lambda e, ps=ps, j=j, ch=ch, t_=t_: e.transpose(ps[:, j * 128:(j + 1) * 128], t_[:, ch * 128:(ch + 1) * 128], ident[:]),
                                 reads=[bt, b_ident], writes=[bps])
                        for j in range(4):
                            ch = half * 4 + j
                            evac(h2T[:, ch, tl * 128:(tl + 1) * 128], ps[:, j * 128:(j + 1) * 128], [bps, b_ops1, b_modT], [b_h2T[tl]],
                                 scale=ops1[:, 8 + ch, 0:1], bias=modT[:, 24 + ch, 0:1])
            qo = [sbt(sq, "qo%d" % i, [128, 512], BF16) for i in range(2)]; b_qo = bufs(2)
            it = 0
            for ck in range(16):
                for tb in range(4):
                    ps, bps = psum()
                    for ch in range(8):
                        mm(ps[:], wqb[:, ch, ck * 128:(ck + 1) * 128], h2T[:, ch, tb * 512:(tb + 1) * 512], ch == 0, ch == 7,
                           [b_wqb[ck // 4]] + b_h2T[tb * 4:(tb + 1) * 4], [bps])
                    q_, bq = qo[it % 2], b_qo[it % 2]
                    it += 1
                    evac(q_[:], ps[:], [bps], [bq])
                    T.dma(sq_r[:, ck, tb * 512:(tb + 1) * 512], q_[:], reads=[bq], writes=[SB["s_qT"][ck * 4 + tb]])

        if "d_x1o" in dbgn:
            T.dma(D["d_x1o"], D["s_x1"], reads=SB["s_x1"], writes=[Buf()])
        if "d_h2T" in dbgn:
            T.dma(D["d_h2T"].rearrange("(c p) n -> p c n", p=128), h2T[:], reads=b_h2T, writes=[Buf()])
        if stage == "pc":
            T.drain()
            return nc

        with Phase(T) as sd:
            k1b = sbt(sd, "k1b", [128, 128], BF16); k2b = sbt(sd, "k2b", [128, 128], BF16); b_kb = Buf()
            K2rep = sbt(sd, "K2rep", [128, 4, 128], BF16); b_K2rep = Buf()
            thr_all = sbt(sd, "thr_all", [128, 16, 8]); bF_all = sbt(sd, "bF_all", [128, 16, 8]); b_thr = bufs(16)
            T.dma(lng[:], D["ln2_g"][0:1, :].partition_broadcast(128), writes=[b_ln])
            T.dma(lnb[:], D["ln2_b"][0:1, :].partition_broadcast(128), writes=[b_ln])
            with Phase(T) as sd1:
                kst = sbt(sd1, "kst", [128, 2, 128]); b_kst = Buf()
                T.dma(kst[:, 0, :], D["keys1T"][:, :], writes=[b_kst])
                T.dma(kst[:, 1, :], D["keys2T"][:, :], writes=[b_kst])
                T.op("dve", lambda e: e.tensor_copy(out=k1b[:], in_=kst[:, 0, :]), reads=[b_kst], writes=[b_kb])
                T.op("dve", lambda e: e.tensor_copy(out=k2b[:], in_=kst[:, 1, :]), reads=[b_kst], writes=[b_kb])
                T.op("dve", lambda e: e.tensor_copy(out=K2rep[:], in_=kst[:, 1:2, :].to_broadcast([128, 4, 128])), reads=[b_kst], writes=[b_K2rep])
            for pss in range(2):
                with Phase(T) as sp_:
                    qTh = sbt(sp_, "qTh", [128, 16, 1024], BF16); b_qTh = Buf()
                    acc = sbt(sp_, "acc", [128, 8, DM]); b_acc = bufs(8)
                    T.dma(qTh[:], sq_r[:, :, pss * 1024:(pss + 1) * 1024], reads=SB["s_qT"], writes=[b_qTh])
                    T.op("pool", lambda e: e.memset(acc[:], 0.0), writes=b_acc)
                    with Phase(T) as sp1:
                        m16 = sbt(sp1, "m16", [128, 16, 16]); b_m16s = bufs(16)
                        tmpk = sbt(sp1, "tmpk", [128, 16, 128]); b_tmpk = bufs(16)
                        tmp2k = sbt(sp1, "tmp2k", [128, 8, 256]); b_tmp2k = bufs(8)
                        b_c16s = bufs(8)
                        tmp = sbt(sp1, "tk_tmp", [128, 128]); b_tmp = Buf()
                        cand = sbt(sp1, "cand", [128, 8, 256]); b_cand = Buf()
                        tmp2 = sbt(sp1, "tk_tmp2", [128, 256]); b_tmp2 = Buf()
                        c16 = sbt(sp1, "c16", [128, 8, 16]); b_c16 = Buf()
                        d16 = sbt(sp1, "d16", [128, 8, 16]); b_d16 = Buf()
                        zz = sbt(sp1, "zz", [128, 8]); b_zz = Buf()
                        for t in range(8):
                            gt = pss * 8 + t
                            banks = [psum() for _ in range(4)]
                            for ck in range(16):
                                ps, bps = banks[ck // 4]
                                mm(ps[:, (ck % 4) * 128:(ck % 4 + 1) * 128], qTh[:, ck, t * 128:(t + 1) * 128], (k1b if ck % 2 == 0 else k2b)[:], True, True,
                                   [b_qTh, b_kb], [bps])
                            srcs = []
                            for ck in range(16):
                                ps, bps = banks[ck // 4]
                                srcs.append((ps[:, (ck % 4) * 128:(ck % 4 + 1) * 128], bps))
                            for ck in range(16):
                                src, bps = srcs[ck]
                                T.op("dve", lambda e, ck=ck, src=src: e.max(out=m16[:, ck, 0:8], in_=src), reads=[bps], writes=[b_m16s[ck]])
                            for ck in range(16):
                                src, bps = srcs[ck]
                                T.op("dve", lambda e, ck=ck, src=src: e.match_replace(out=tmpk[:, ck, :], in_to_replace=m16[:, ck, 0:8], in_values=src, imm_value=-1e30),
                                     reads=[bps, b_m16s[ck]], writes=[b_tmpk[ck]])
                            for ck in range(16):
                                T.op("dve", lambda e, ck=ck: e.max(out=m16[:, ck, 8:16], in_=tmpk[:, ck, :]), reads=[b_tmpk[ck]], writes=[b_m16s[ck]])
                            m16v = m16[:].rearrange("p (h two) k -> p h two k", two=2)
                            T.op("dve", lambda e, m16v=m16v: e.tensor_tensor(
                                out=cand[:].rearrange("p h (a b) -> p h a b", a=16),
                                in0=m16v[:, :, 0, :].unsqueeze(3).to_broadcast([128, 8, 16, 16]),
                                in1=m16v[:, :, 1, :].unsqueeze(2).to_broadcast([128, 8, 16, 16]), op=ALU.add), reads=b_m16s, writes=[b_cand])
                            for h in range(8):
                                T.op("dve", lambda e, h=h: e.max(out=c16[:, h, 0:8], in_=cand[:, h, :]), reads=[b_cand], writes=[b_c16s[h]])
                            for h in range(8):
                                T.op("dve", lambda e, h=h: e.match_replace(out=tmp2k[:, h, :], in_to_replace=c16[:, h, 0:8], in_values=cand[:, h, :], imm_value=-1e30),
                                     reads=[b_cand, b_c16s[h]], writes=[b_tmp2k[h]])
                            for h in range(8):
                                T.op("dve", lambda e, h=h: e.max(out=c16[:, h, 8:16], in_=tmp2k[:, h, :]), reads=[b_tmp2k[h]], writes=[b_c16s[h]])
                            T.op("dve", lambda e: e.tensor_tensor(out=d16[:], in0=c16[:], in1=c16[:, :, 0:1].to_broadcast([128, 8, 16]), op=ALU.subtract),
                                 reads=b_c16s, writes=[b_d16])
                            T.op("act", lambda e: e.activation(out=d16[:], in_=d16[:], func=AF.Exp), reads=[b_d16], writes=[b_d16])
                            T.op("dve", lambda e: e.tensor_reduce(out=zz[:], in_=d16[:], axis=AX.X, op=ALU.add), reads=[b_d16], writes=[b_zz])
                            T.op("act", lambda e: e.activation(out=zz[:], in_=zz[:], func=AF.Ln), reads=[b_zz], writes=[b_zz])
                            T.op("dve", lambda e: e.tensor_tensor(out=zz[:], in0=zz[:], in1=c16[:, :, 0], op=ALU.add), reads=[b_zz] + b_c16s, writes=[b_zz])
                            T.op("dve", lambda e, gt=gt: e.tensor_scalar(out=bF_all[:, gt, :], in0=zz[:], scalar1=-1.0, scalar2=None, op0=ALU.mult), reads=[b_zz], writes=[b_thr[gt]])
                            T.op("dve", lambda e, gt=gt: e.tensor_scalar(out=thr_all[:, gt, :], in0=c16[:, :, 15], scalar1=-1e-5, scalar2=None, op0=ALU.add), reads=b_c16s, writes=[b_thr[gt]])

                    with Phase(T) as sp3:
                        stg = sbt(sp3, "stg", [128, 8, 512]); b_stg = Buf()
                        stg_v = stg[:].rearrange("p a b -> p (a b)").rearrange("p (c d) -> p c d", c=4)
                        UTb = [sbt(sp3, "UTb%d" % i, [128, 8, 512], BF16) for i in range(2)]; b_UTb = bufs(2)
                        Vb = [sbt(sp3, "Vb%d" % i, [128, 4, DM], BF16) for i in range(2)]; b_Vb = bufs(2)
                        K1s = [sbt(sp3, "K1s%d" % i, [128, 4, 128], BF16) for i in range(2)]; b_K1s = bufs(2)
                        Ag = [sbt(sp3, "Ag%d" % i, [128, 512], BF16) for i in range(2)]; b_Ag = bufs(2)
                        AGT = [sbt(sp3, "AGT%d" % i, [128, 512], BF16) for i in range(2)]; b_AGT = bufs(2)
                        Hs = sbt(sp3, "Hs", [128, 8, 512], BF16); b_Hs = bufs(8)
                        As = [sbt(sp3, "As%d" % i, [128, 8, 512], BF16) for i in range(2)]; b_As = bufs(2)
                        EF3 = [sbt(sp3, "EFx%d" % i, [128, 512], BF16) for i in range(3)]; b_EF3 = bufs(3)
                        Gm4 = [sbt(sp3, "Gmx%d" % i, [128, 512], BF16) for i in range(4)]; b_Gm4 = bufs(4)
                        uT_r = D["peer_uT"].rearrange("(ch p) n -> p ch n", p=128)
                        v_r = D["peer_v"].rearrange("(g c p) d -> g p c d", c=4, p=128)
                        K2flat = K2rep[:].rearrange("p a b -> p (a b)")

                        def load_U(g):
                            i = g % 2
                            T.dma(stg[:], uT_r[:, :, g * 512:(g + 1) * 512], writes=[b_stg])
                            T.op("pool", lambda e: e.tensor_copy(out=UTb[i][:], in_=stg[:]), reads=[b_stg], writes=[b_UTb[i]])

                        def load_V(g):
                            i = g % 2
                            T.dma(stg_v, v_r[g], writes=[b_stg])
                            T.op("pool", lambda e: e.tensor_copy(out=Vb[i][:], in_=stg_v), reads=[b_stg], writes=[b_Vb[i]])
                            T.op("pool", lambda e: e.tensor_copy(out=K1s[i][:], in_=k1b[:, 4 * g:4 * g + 4].unsqueeze(2).to_broadcast([128, 4, 128])), reads=[b_kb], writes=[b_K1s[i]])

                        dcnt = [0]
                        dbank = {}

                        def emit_D(g, t, h):
                            bk = (1, 2, 3, 0)[dcnt[0] % 4]
                            dcnt[0] += 1
                            dbank[(g, t, h)] = bk
                            psD, bpD = PSB[bk], PSBUF[bk]
                            mm(psD[:], qTh[:, 2 * h + 1, t * 128:(t + 1) * 128], K2flat, True, False, [b_qTh, b_K2rep], [bpD])
                            mm(psD[:], qTh[:, 2 * h, t * 128:(t + 1) * 128], K1s[g % 2][:].rearrange("p a b -> p (a b)"), False, True, [b_qTh, b_K1s[g % 2]], [bpD])

                        def make_tail(g, t, itn):
                            gi = g % 2
                            ag, bag = Ag[itn % 2], b_Ag[itn % 2]
                            agt, bagt = AGT[itn % 2], b_AGT[itn % 2]
                            psT, bpT = PSB[4][:].bitcast(BF16), PSBUF[4]

                            def s_tr():
                                for c in range(4):
                                    T.op("pe", lambda e, c=c: e.transpose(psT[:, c * 128:(c + 1) * 128], ag[:, c * 128:(c + 1) * 128], identb[:]),
                                         reads=[bag, b_identb], writes=[bpT])

                            def s_cp():
                                T.op("act", lambda e: e.copy(out=agt[:], in_=psT[:, 0:512]), reads=[bpT], writes=[bagt])

                            def s_v(hf):
                                def f():
                                    psO, bpO = PSB[6 + hf], PSBUF[6 + hf]
                                    for c in range(4):
                                        mm(psO[:], agt[:, c * 128:(c + 1) * 128], Vb[gi][:, c, hf * 512:(hf + 1) * 512], c == 0, c == 3, [bagt, b_Vb[gi]], [bpO])
                                return f

                            def s_acc(hf):
                                def f():
                                    psO, bpO = PSB[6 + hf], PSBUF[6 + hf]
                                    T.op("dve", lambda e: e.tensor_tensor(out=acc[:, t, hf * 512:(hf + 1) * 512], in0=acc[:, t, hf * 512:(hf + 1) * 512],
                                                                          in1=psO[:], op=ALU.add), reads=[bpO, b_acc[t]], writes=[b_acc[t]])
                                return f
                            return {0: [s_tr], 2: [s_cp], 3: [s_v(0)], 4: [s_v(1)], 5: [s_acc(0)], 6: [s_acc(1)]}

                        def gelu_group(g):
                            T.op("act", lambda e: e.activation(out=As[g % 2][:], in_=Hs[:], func=AF.Gelu), reads=b_Hs, writes=[b_As[g % 2]])

                        load_U(0)
                        load_V(0)
                        for t in range(8):
                            gt = pss * 8 + t
                            psH, bpH = PSB[0], PSBUF[0]
                            for ch in range(8):
                                mm(psH[:], h2T[:, ch, gt * 128:(gt + 1) * 128], UTb[0][:, ch, :], ch == 0, ch == 7, [b_h2T[gt], b_UTb[0]], [bpH])
                            T.op("act", lambda e, t=t, psH=psH: e.copy(out=Hs[:, t, :], in_=psH[:]), reads=[bpH], writes=[b_Hs[t]])
                        gelu_group(0)
                        load_U(1)
                        pending = {}
                        itn = 0
                        efn = 0
                        for g in range(32):
                            gi = g % 2
                            for t in range(8):
                                gt = pss * 8 + t
                                for h in range(4):
                                    emit_D(g, t, h)
                                psH, bpH = PSB[4], PSBUF[4]
                                psG, bpG = PSB[5], PSBUF[5]
                                for h in range(8):
                                    bk = dbank[(g, t, h)]
                                    psD, bpD = PSB[bk], PSBUF[bk]
                                    ef, bef = EF3[efn % 3], b_EF3[efn % 3]
                                    gm, bgm = Gm4[efn % 4], b_Gm4[efn % 4]
                                    efn += 1
                                    T.op("act", lambda e, ef=ef, psD=psD, gt=gt, h=h: e.activation(out=ef[:], in_=psD[:], func=AF.Exp, bias=bF_all[:, gt, h:h + 1]),
                                         reads=[bpD, b_thr[gt]], writes=[bef])
                                    T.op("dve", lambda e, gm=gm, psD=psD, ef=ef, gt=gt, h=h: e.scalar_tensor_tensor(out=gm[:], in0=psD[:], scalar=thr_all[:, gt, h:h + 1], in1=ef[:],
                                                                                                                      op0=ALU.is_ge, op1=ALU.mult), reads=[bpD, bef, b_thr[gt]], writes=[bgm])
                                    for f in pending.get(h, []):
                                        f()
                                    if h == 1 and t == 1:
                                        if g + 1 < 32:
                                            load_V(g + 1)
                                        if g + 2 < 32:
                                            load_U(g + 2)
                                    if g + 1 < 32 and h >= 4:
                                        for hh in (2 * (h - 4), 2 * (h - 4) + 1):
                                            mm(psH[:], h2T[:, hh, gt * 128:(gt + 1) * 128], UTb[(g + 1) % 2][:, hh, :], hh == 0, hh == 7, [b_h2T[gt], b_UTb[(g + 1) % 2]], [bpH])
                                    if h + 4 < 8:
                                        emit_D(g, t, h + 4)
                                    mm(psG[:], identb[:], gm[:], h == 0, h == 7, [b_identb, bgm], [bpG])
                                if g + 1 < 32:
                                    T.op("act", lambda e, t=t, psH=psH: e.copy(out=Hs[:, t, :], in_=psH[:]), reads=[bpH], writes=[b_Hs[t]])
                                ag, bag = Ag[itn % 2], b_Ag[itn % 2]
                                T.op("dve", lambda e, ag=ag, psG=psG, gi=gi, t=t: e.tensor_tensor(out=ag[:], in0=As[gi][:, t, :], in1=psG[:], op=ALU.mult),
                                     reads=[b_As[gi], bpG], writes=[bag])
                                pending = make_tail(g, t, itn)
                                itn += 1
                            if g + 1 < 32:
                                gelu_group(g + 1)
                        for h in range(8):
                            for f in pending.get(h, []):
                                f()
                    if "d_pm" in dbgn:
                        T.dma(D["d_pm"][pss * 1024:(pss + 1) * 1024, :].rearrange("(t p) d -> p t d", p=128), acc[:], reads=b_acc, writes=[Buf()])
                    with Phase(T) as sp2:
                        x1l = [sbt(sp2, "x1l%d" % i, [128, DM]) for i in range(2)]; b_x1l = bufs(2)
                        scr = sbt(sp2, "lnscr2", [128, 2, 6]); b_scr = Buf()
                        mv = sbt(sp2, "lnmv2", [128, 4]); b_mv = Buf()
                        for t in range(8):
                            gt = pss * 8 + t
                            x_, bx = x1l[t % 2], b_x1l[t % 2]
                            T.dma(x_[:], D["s_x1"][gt * 128:(gt + 1) * 128, :], reads=[SB["s_x1"][gt]], writes=[bx])
                            T.op("dve", lambda e, t=t: e.tensor_tensor(out=acc[:, t, :], in0=acc[:, t, :], in1=g2b[:], op=ALU.mult), reads=[b_acc[t], b_g2b], writes=[b_acc[t]])
                            T.op("dve", lambda e, t=t, x_=x_: e.scalar_tensor_tensor(out=acc[:, t, :], in0=x_[:], scalar=ALPHA, in1=acc[:, t, :], op0=ALU.mult, op1=ALU.add),
                                 reads=[bx, b_acc[t]], writes=[b_acc[t]])
                            layer_norm_tile(acc[:, t, :], b_acc[t], scr, b_scr, mv, b_mv)
                            T.dma(D["out"][gt * 128:(gt + 1) * 128, :], acc[:, t, :], reads=[b_acc[t]], writes=[Buf()])
            if "d_thr" in dbgn:
                T.dma(D["d_thr"], thr_all[:].rearrange("p a b -> p (a b)"), reads=b_thr, writes=[Buf()])
                T.dma(D["d_bF"], bF_all[:].rearrange("p a b -> p (a b)"), reads=b_thr, writes=[Buf()])
        T.drain()
    return nc


_CONST = {}


def _bf(a):
    return np.ascontiguousarray(a.astype(ml_dtypes.bfloat16))


def _constants():
    if _CONST:
        return _CONST
    n = SEQ
    t = np.linspace(0.0, 1.0, n, dtype=np.float32)[:, None]
    w = (2.0 * math.pi * np.arange(n, dtype=np.float32)[:, None] / n).astype(np.float32)
    bands = np.linspace(1e-4, 7, 8, dtype=np.float32)[None, :]
    z = np.concatenate([t, np.cos(bands * w), -np.sin(bands * w)], axis=-1).astype(np.float32)
    _CONST["zT"] = np.ascontiguousarray(z.T)
    min_decay = math.log(1e-2) / 1.5
    max_decay = math.log(1e-2) / 0.3
    deltas = np.abs(np.linspace(min_decay, max_decay, 512, dtype=np.float32))
    _CONST["window"] = (np.exp(-t * deltas[None, :]) + np.float32(0.05)).astype(np.float32)
    a = np.arange(4096, dtype=np.int64)
    ab = (a[:, None] * a[None, :]) % NFFT
    ang = ab.astype(np.float64) * (2.0 * math.pi / NFFT)
    Mc = np.cos(ang).astype(np.float32)
    Ms = np.sin(ang).astype(np.float32)
    alt = np.where(a % 2 == 0, 1.0, -1.0).astype(np.float32)
    nyq = np.zeros((128, 32, 128), np.float32)
    nyq[:, :, 0] = alt.reshape(32, 128).T
    _CONST["nyq_fwd"] = _bf(nyq.reshape(128, 4096))
    _CONST["altrow"] = _bf(alt[None, :])
    del ang, ab
    for nm, M in (("c", Mc), ("s", Ms)):
        Mb = M.astype(ml_dtypes.bfloat16)
        T1 = Mb.reshape(32, 128, 32, 128).transpose(2, 1, 0, 3)
        _CONST["dft_" + nm] = np.ascontiguousarray(T1).reshape(32, 128, 4096)
        for half in range(2):
            sub = Mb[:, half * 2048:(half + 1) * 2048].reshape(32, 128, 4, 512).transpose(2, 1, 0, 3)
            _CONST["dfto_%s%d" % (nm, half)] = np.ascontiguousarray(sub).reshape(4, 128, 16384)
    inv = (10000.0 ** (-np.arange(16, dtype=np.float32) / 16)).astype(np.float32)
    for half in range(2):
        pos = half * OWN - 128 + np.arange(EXT)
        valid = ((pos >= 0) & (pos < SEQ)).astype(np.float32)
        posc = np.clip(pos, 0, SEQ - 1)
        row = (posc // 64).astype(np.float32)
        col = (posc % 64).astype(np.float32)
        ar = row[None, :] * inv[:, None]
        ac = col[None, :] * inv[:, None]
        rc = np.concatenate([np.cos(ar), np.cos(ar), np.cos(ac), np.cos(ac)], axis=0).astype(np.float32)
        rs = np.concatenate([-np.sin(ar), np.sin(ar), -np.sin(ac), np.sin(ac)], axis=0).astype(np.float32)
        _CONST["rope_c%d" % half] = np.ascontiguousarray(rc)
        _CONST["rope_s%d" % half] = np.ascontiguousarray(rs)
        v2 = np.concatenate([valid[0:128], valid[EXT - 128:EXT]])[None, :].repeat(128, axis=0)
        _CONST["validT%d" % half] = np.ascontiguousarray(v2.astype(np.float32))
        kj = np.arange(128)[:, None]
        qi = np.arange(128)[None, :]
        mp = np.zeros((128, 16, 128), np.float32)
        mn = np.zeros((128, 16, 128), np.float32)
        for qb in range(16):
            gb = half * 16 + qb
            if gb - 1 >= 0:
                mp[:, qb, :] = (kj >= qi)
            if gb + 1 < 32:
                mn[:, qb, :] = (kj <= qi)
        _CONST["mask_prev%d" % half] = _bf(mp.reshape(128, 2048))
        _CONST["mask_next%d" % half] = _bf(mn.reshape(128, 2048))
    return _CONST


def _colT(v, ncols):
    return np.ascontiguousarray(np.asarray(v, np.float32).reshape(ncols, 128).T)


def prep_inputs(inputs, cores=range(N_CORES)):
    C = _constants()
    g = {k: np.asarray(v) for k, v in inputs.items()}
    perm = np.concatenate([np.arange(16, 32), np.arange(0, 16), np.arange(48, 64), np.arange(32, 48)])
    w_in = np.ascontiguousarray(g["w_in"][0])
    cols = (1536 + (np.arange(10)[:, None] * 64 + perm[None, :])).reshape(-1)
    shared = {
        "ctxT": None, "cctxT": _colT(g["c_ctx"], 8),
        "w_mod": np.ascontiguousarray(g["w_mod"][0]), "b_modT": _colT(g["b_mod"][0], 48),
        "b_mod_row": np.ascontiguousarray(g["b_mod"][0][None, :]),
        "w_in": w_in, "w_in_perm": np.ascontiguousarray(w_in[:, cols]),
        "conv_w": np.ascontiguousarray(g["hy_conv_w"][0]), "conv_b": np.ascontiguousarray(g["hy_conv_b"][0][None, :]),
        "conv_bT": _colT(g["hy_conv_b"][0], 12),
        "f_w1": np.ascontiguousarray(g["hy_f_w1"][0]), "f_b1": np.ascontiguousarray(g["hy_f_b1"][0][:, None]),
        "f_f1": np.ascontiguousarray(g["hy_f_freq1"][0][:, None]),
        "f_w2": np.ascontiguousarray(g["hy_f_w2"][0]), "f_b2": np.ascontiguousarray(g["hy_f_b2"][0][:, None]),
        "f_f2": np.ascontiguousarray(g["hy_f_freq2"][0][:, None]),
        "f_w3": np.ascontiguousarray(g["hy_f_w3"][0]), "f_b3": np.ascontiguousarray(g["hy_f_b3"][0][None, :]),
        "hy_skip": np.ascontiguousarray(g["hy_skip"][0]),
        "zT": C["zT"], "window": C["window"], "nyq_fwd": C["nyq_fwd"], "altrow": C["altrow"], "dft_c": C["dft_c"], "dft_s": C["dft_s"],
        "sink_row": np.ascontiguousarray(g["attn_sink"][0][None, :]),
        "w_out": np.ascontiguousarray(g["w_out"][0]),
        "ln1_g": np.ascontiguousarray(g["ln1_g"]), "ln1_b": np.ascontiguousarray(g["ln1_b"]),
        "ln2_g": np.ascontiguousarray(g["ln2_g"]), "ln2_b": np.ascontiguousarray(g["ln2_b"]),
        "peer_wq": np.ascontiguousarray(g["peer_wq"][0]),
        "keys1T": np.ascontiguousarray(g["peer_keys1"][0].T), "keys2T": np.ascontiguousarray(g["peer_keys2"][0].T),
        "peer_uT": np.ascontiguousarray(g["peer_u"][0].T), "peer_v": np.ascontiguousarray(g["peer_v"][0]),
    }
    maps = []
    xT_cache = {}
    for core in cores:
        b, half = core // 2, core % 2
        if b not in xT_cache:
            xT_cache[b] = np.ascontiguousarray(g["x"][b].T)
        xT = xT_cache[b]
        s0 = half * OWN - 128
        xe = np.zeros((DM, EXT), np.float32)
        lo, hi = max(s0, 0), min(s0 + EXT, SEQ)
        xe[:, lo - s0:hi - s0] = xT[:, lo:hi]
        m = dict(shared)
        m.update({
            "xT_full": xT, "xT_ext": xe, "x_own": np.ascontiguousarray(g["x"][b, half * OWN:(half + 1) * OWN]),
            "ctxT": np.ascontiguousarray(g["ctx"][b].T), "cT": _colT(g["c"][b], 8),
            "dfto_c": C["dfto_c%d" % half], "dfto_s": C["dfto_s%d" % half],
            "rope_c": C["rope_c%d" % half], "rope_s": C["rope_s%d" % half],
            "mask_prev": C["mask_prev%d" % half], "mask_next": C["mask_next%d" % half], "validT": C["validT%d" % half],
        })
        maps.append(m)
    return maps


_NC = {}


def kernel(**inputs):
    if "nc" not in _NC:
        _NC["nc"] = build("full")
    maps = prep_inputs(inputs)
    res = run_bass_kernel_spmd(_NC["nc"], maps, core_ids=list(range(N_CORES)))
    out = np.zeros((4, SEQ, DM), np.float32)
    for core in range(N_CORES):
        b, half = core // 2, core % 2
        out[b, half * OWN:(half + 1) * OWN] = np.asarray(res.results[core]["out"], np.float32)
    return out
```

```python
import math
from contextlib import ExitStack

import numpy as np
import ml_dtypes
import concourse.bass as bass
import concourse.mybir as mybir
from concourse.bass_utils import run_bass_kernel_spmd

F32 = mybir.dt.float32
BF16 = mybir.dt.bfloat16
AF = mybir.ActivationFunctionType
ALU = mybir.AluOpType
AX = mybir.AxisListType

N_CORES = 8
SEQ = 4096
DM = 1024
OWN = 2048
EXT = 2304
NFFT = 8192
LN_EPS = 1e-5
ALPHA = 2.0 ** 0.25
MAGIC = 12582912.0


class Buf:
    __slots__ = ("w", "r")

    def __init__(self):
        self.w = None
        self.r = {}


class Phase(ExitStack):
    def __init__(self, trk):
        super().__init__()
        self._trk = trk

    def __exit__(self, *a):
        if a[0] is None:
            self._trk.barrier()
        return super().__exit__(*a)


def bufs(n):
    return [Buf() for _ in range(n)]


class Trk:
    def __init__(self, nc, stack, n_dma_sems=32):
        self.nc = nc
        self.eng = {"pe": nc.tensor, "act": nc.scalar, "dve": nc.vector, "pool": nc.gpsimd, "sp": nc.sync}
        self.sems = {}
        self.cnt = {}
        for k in ["pe", "act", "dve", "pool"]:
            self.sems[k] = stack.enter_context(nc.semaphore("s_" + k))
            self.cnt[k] = 0
        self.dsems = []
        for i in range(n_dma_sems):
            key = "d%d" % i
            self.sems[key] = stack.enter_context(nc.semaphore("s_" + key))
            self.cnt[key] = 0
            self.dsems.append(key)
        self.dnext = 0
        self.waited = {k: {} for k in self.eng}
        self.n_inst = 0

    def _wait(self, e, key, val):
        if self.waited[e].get(key, 0) >= val:
            return
        self.eng[e].wait_ge(self.sems[key], val)
        self.waited[e][key] = val
        self.n_inst += 1

    def _deps(self, e, reads, writes):
        deps = {}
        for b in reads:
            if b.w is not None and deps.get(b.w[0], 0) < b.w[1]:
                deps[b.w[0]] = b.w[1]
        for b in writes:
            if b.w is not None and deps.get(b.w[0], 0) < b.w[1]:
                deps[b.w[0]] = b.w[1]
            for k, v in b.r.items():
                if deps.get(k, 0) < v:
                    deps[k] = v
        for k, v in deps.items():
            if k == e and e == "pe":
                continue
            self._wait(e, k, v)

    def _mark(self, tok, reads, writes):
        k, v = tok
        for b in writes:
            b.w = tok
            b.r = {}
        for b in reads:
            if b.r.get(k, 0) < v:
                b.r[k] = v

    def op(self, e, fn, reads=(), writes=()):
        self._deps(e, reads, writes)
        inst = fn(self.eng[e])
        self.cnt[e] += 1
        inst.then_inc(self.sems[e], 1)
        self._mark((e, self.cnt[e]), reads, writes)
        self.n_inst += 1
        return inst

    def dma(self, out, in_, reads=(), writes=(), q="sp"):
        key = self.dsems[self.dnext]
        self.dnext = (self.dnext + 1) % len(self.dsems)
        if self.cnt[key] > 0:
            self._wait(q, key, self.cnt[key])
        self._deps(q, reads, writes)
        inst = self.eng[q].dma_start(out=out, in_=in_)
        self.cnt[key] += 16
        inst.then_inc(self.sems[key], 16)
        self._mark((key, self.cnt[key]), reads, writes)
        self.n_inst += 1
        return inst

    def barrier(self):
        for e in ["pe", "act", "dve", "pool", "sp"]:
            for k in ["pe", "act", "dve", "pool"] + self.dsems:
                if self.cnt[k] > 0 and not (k == e and e == "pe"):
                    self._wait(e, k, self.cnt[k])

    def drain(self, q="sp"):
        for k in self.dsems:
            if self.cnt[k] > 0:
                self._wait(q, k, self.cnt[k])
        for k in ["pe", "act", "dve", "pool"]:
            if self.cnt[k] > 0:
                self._wait(q, k, self.cnt[k])


INPUT_SPECS = {}


def build(stage="full", dbg=()):
    nc = bass.Bass("TRN2", target_bir_lowering=False)
    D = {}

    def din(name, shape, dt=F32):
        D[name] = nc.dram_tensor(name, list(shape), dt, kind="ExternalInput").ap()
        INPUT_SPECS[name] = (tuple(shape), dt)

    def dscr(name, shape, dt=F32):
        D[name] = nc.dram_tensor(name, list(shape), dt).ap()

    def dout(name, shape, dt=F32):
        D[name] = nc.dram_tensor(name, list(shape), dt, kind="ExternalOutput").ap()

    din("xT_full", [DM, SEQ]); din("xT_ext", [DM, EXT]); din("x_own", [OWN, DM]); din("ctxT", [DM, 256])
    din("cT", [128, 8]); din("cctxT", [128, 8])
    din("w_mod", [DM, 6144]); din("b_modT", [128, 48]); din("b_mod_row", [1, 6144])
    din("w_in", [DM, 2304]); din("w_in_perm", [DM, 640])
    din("conv_w", [3, 1536]); din("conv_b", [1, 1536]); din("conv_bT", [128, 12])
    din("f_w1", [17, 64]); din("f_b1", [64, 1]); din("f_f1", [64, 1])
    din("f_w2", [64, 64]); din("f_b2", [64, 1]); din("f_f2", [64, 1])
    din("f_w3", [64, 2048]); din("f_b3", [1, 2048]); din("hy_skip", [2, 512])
    din("zT", [17, SEQ]); din("window", [SEQ, 512])
    din("dft_c", [32, 128, 4096], BF16); din("dft_s", [32, 128, 4096], BF16)
    din("nyq_fwd", [128, 4096], BF16); din("altrow", [1, 4096], BF16)
    din("dfto_c", [4, 128, 16384], BF16); din("dfto_s", [4, 128, 16384], BF16)
    din("rope_c", [64, EXT]); din("rope_s", [64, EXT])
    din("mask_prev", [128, 16 * 128], BF16); din("mask_next", [128, 16 * 128], BF16); din("validT", [128, 256])
    din("sink_row", [1, 8])
    din("w_out", [DM, DM]); din("ln1_g", [1, DM]); din("ln1_b", [1, DM]); din("ln2_g", [1, DM]); din("ln2_b", [1, DM])
    din("peer_wq", [DM, 2048]); din("keys1T", [128, 128]); din("keys2T", [128, 128])
    din("peer_uT", [DM, 16384]); din("peer_v", [16384, DM])
    dout("out", [OWN, DM])
    for nm, shp, dt_ in dbg:
        dout(nm, shp, dt_)
    dbgn = {nm for nm, _, _ in dbg}
    dscr("s_x2T", [512, OWN], BF16); dscr("s_attT", [64, 8 * OWN], BF16)
    dscr("s_spec", [2, 32, 128, 1024]); dscr("s_x1", [OWN, DM]); dscr("s_hyT", [512, OWN], BF16); dscr("s_qT", [128, 16 * OWN], BF16)

    SB = {"s_x2T": bufs(16), "s_attT": bufs(32), "s_spec": bufs(64), "s_x1": bufs(16), "s_hyT": bufs(16), "s_qT": bufs(64)}
    with ExitStack() as st:
        T = Trk(nc, st)
        PSB = [st.enter_context(nc.psum_tensor("ps%d" % i, [128, 512], F32)) for i in range(8)]
        PSBUF = bufs(8)
        psn = [0]

        def psum():
            i = psn[0] % 8
            psn[0] += 1
            return PSB[i], PSBUF[i]

        uniq = [0]

        def sbt(stack, name, shape, dt=F32):
            uniq[0] += 1
            return stack.enter_context(nc.sbuf_tensor("%s_%d" % (name, uniq[0]), list(shape), dt))

        def mm(ps, lhsT, rhs, start, stop, reads, writes):
            T.op("pe", lambda e: e.matmul(ps, lhsT=lhsT, rhs=rhs, start=start, stop=stop), reads=reads, writes=writes)

        evn = [0]

        def evac(out, in_, reads, writes, func=None, scale=None, bias=None):
            if func is not None or scale is not None or bias is not None:
                kw = {}
                if scale is not None:
                    kw["scale"] = scale
                if bias is not None:
                    kw["bias"] = bias
                T.op("act", lambda e: e.activation(out=out, in_=in_, func=func or AF.Identity, **kw), reads=reads, writes=writes)
                return
            evn[0] += 1
            if evn[0] % 2:
                T.op("act", lambda e: e.copy(out=out, in_=in_), reads=reads, writes=writes)
            else:
                T.op("dve", lambda e: e.tensor_copy(out=out, in_=in_), reads=reads, writes=writes)

        ident = sbt(st, "ident", [128, 128]); b_ident = Buf()
        identb = sbt(st, "identb", [128, 128], BF16); b_identb = Buf()
        onesb = sbt(st, "onesb", [128, 128], BF16); b_onesb = Buf()
        modT = sbt(st, "modT", [128, 48, 2]); b_modT = Buf()
        ops1 = sbt(st, "ops1", [128, 16, 2]); b_ops1 = Buf()
        g1b = sbt(st, "g1b", [128, DM]); b_g1b = Buf()
        g2b = sbt(st, "g2b", [128, DM]); b_g2b = Buf()
        epsc = sbt(st, "epsc", [128, 1]); b_epsc = Buf()
        T.op("pool", lambda e: e.memset(ident[:], 1.0), writes=[b_ident])
        T.op("pool", lambda e: e.affine_select(out=ident[:], in_=ident[:], pattern=[[-1, 128]], compare_op=ALU.is_equal,
                                               fill=0.0, base=0, channel_multiplier=1), reads=[b_ident], writes=[b_ident])
        T.op("dve", lambda e: e.tensor_copy(out=identb[:], in_=ident[:]), reads=[b_ident], writes=[b_identb])
        T.op("pool", lambda e: e.memset(onesb[:], 1.0), writes=[b_onesb])
        T.op("pool", lambda e: e.memset(epsc[:], LN_EPS), writes=[b_epsc])

        with Phase(T) as s0:
            sc = sbt(s0, "sc", [128, 2, 8]); b_sc = Buf()
            screp = sbt(s0, "screp", [128, 8, 128]); b_screp = Buf()
            bmT = sbt(s0, "bmT", [128, 48]); b_bmT = Buf()
            wst = [sbt(s0, "wmst%d" % i, [128, 8, 512]) for i in range(2)]; b_wst = bufs(2)
            T.dma(sc[:, 0, :], D["cT"][:, :], writes=[b_sc])
            T.dma(sc[:, 1, :], D["cctxT"][:, :], writes=[b_sc])
            T.dma(bmT[:], D["b_modT"][:, :], writes=[b_bmT])
            T.op("act", lambda e: e.activation(out=sc[:], in_=sc[:], func=AF.Silu), reads=[b_sc], writes=[b_sc])
            for ch in range(8):
                T.op("dve", lambda e, ch=ch: e.tensor_scalar(out=screp[:, ch, :], in0=ident[:], scalar1=0.0, scalar2=sc[:, 0, ch:ch + 1],
                                                              op0=ALU.mult, op1=ALU.add), reads=[b_ident, b_sc], writes=[b_screp])
            T.dma(g1b[:], D["b_mod_row"][0:1, 2048:3072].partition_broadcast(128), writes=[b_g1b])
            T.dma(g2b[:], D["b_mod_row"][0:1, 5120:6144].partition_broadcast(128), writes=[b_g2b])
            wm = D["w_mod"].rearrange("(ch p) n -> p ch n", p=128)
            psm, b_psm = psum()
            for blk in range(12):
                w = wst[blk % 2]; bw = b_wst[blk % 2]
                T.dma(w[:], wm[:, :, blk * 512:(blk + 1) * 512], writes=[bw])
                for j in range(4):
                    col = (blk * 4 + j) * 2
                    for ch in range(8):
                        mm(psm[:, col:col + 2], w[:, ch, j * 128:(j + 1) * 128], sc[:, :, ch], ch == 0, ch == 7,
                           [bw, b_sc], [b_psm])
                if blk in (4, 5, 10, 11):
                    psg, b_psg = psum()
                    for ch in range(8):
                        mm(psg[:], screp[:, ch, :], w[:, ch, :], ch == 0, ch == 7, [bw, b_screp], [b_psg])
                    gb, bgb = (g1b, b_g1b) if blk in (4, 5) else (g2b, b_g2b)
                    off = (blk % 2) * 512
                    T.op("dve", lambda e, gb=gb, off=off, psg=psg: e.tensor_tensor(out=gb[:, off:off + 512], in0=gb[:, off:off + 512],
                                                                                   in1=psg[:], op=ALU.add), reads=[b_psg, bgb], writes=[bgb])
            T.op("dve", lambda e: e.tensor_tensor(out=modT[:], in0=psm[:, 0:96].rearrange("p (a b) -> p a b", b=2),
                                                  in1=bmT[:].unsqueeze(2).to_broadcast([128, 48, 2]), op=ALU.add),
                 reads=[b_psm, b_bmT], writes=[b_modT])
            T.op("dve", lambda e: e.tensor_scalar(out=ops1[:, 0:8, :], in0=modT[:, 8:16, :], scalar1=1.0, scalar2=None, op0=ALU.add),
                 reads=[b_modT], writes=[b_ops1])
            T.op("dve", lambda e: e.tensor_scalar(out=ops1[:, 8:16, :], in0=modT[:, 32:40, :], scalar1=1.0, scalar2=None, op0=ALU.add),
                 reads=[b_modT], writes=[b_ops1])

        def dbg_dump(name, src_ap, rb):
            if name in dbgn:
                T.dma(D[name], src_ap, reads=[rb], writes=[Buf()])

        dbg_dump("d_modT", modT[:].rearrange("p a b -> p (a b)"), b_modT)
        dbg_dump("d_g1b", g1b[:], b_g1b)

        if stage == "p0":
            T.drain()
            return nc

        with Phase(T) as sa:
            hTe = sbt(sa, "hTe", [128, 8, EXT + 2], BF16); b_hTe = bufs(8)
            hcT = sbt(sa, "hcT", [128, 8, 256], BF16); b_hcT = Buf()
            valid = sbt(sa, "valid", [128, 256]); b_valid = Buf()
            T.dma(valid[:], D["validT"][:, :], writes=[b_valid])
            with Phase(T) as s1:
                xst = [sbt(s1, "xst%d" % i, [128, EXT]) for i in range(2)]; b_xst = bufs(2)
                xte = D["xT_ext"].rearrange("(ch p) n -> p ch n", p=128)
                cte = D["ctxT"].rearrange("(ch p) n -> p ch n", p=128)
                for ch in range(8):
                    x_, bx = xst[ch % 2], b_xst[ch % 2]
                    T.dma(x_[:], xte[:, ch, :], writes=[bx])
                    evac(hTe[:, ch, 0:EXT], x_[:], [bx, b_ops1, b_modT], [b_hTe[ch]], scale=ops1[:, ch, 0:1], bias=modT[:, ch, 0:1])
                    T.op("dve", lambda e, ch=ch: e.tensor_tensor(out=hTe[:, ch, 0:128], in0=hTe[:, ch, 0:128], in1=valid[:, 0:128], op=ALU.mult),
                         reads=[b_hTe[ch], b_valid], writes=[b_hTe[ch]])
                    T.op("dve", lambda e, ch=ch: e.tensor_tensor(out=hTe[:, ch, EXT - 128:EXT], in0=hTe[:, ch, EXT - 128:EXT], in1=valid[:, 128:256], op=ALU.mult),
                         reads=[b_hTe[ch], b_valid], writes=[b_hTe[ch]])
                for ch in range(8):
                    x_, bx = xst[ch % 2], b_xst[ch % 2]
                    T.dma(x_[:, 0:256], cte[:, ch, :], writes=[bx])
                    evac(hcT[:, ch, :], x_[:, 0:256], [bx, b_ops1, b_modT], [b_hcT], scale=ops1[:, ch, 1:2], bias=modT[:, ch, 1:2])

            win_r = D["w_in"].rearrange("(ch p) n -> p ch n", p=128)
            with Phase(T) as s2:
                wsg = sbt(s2, "wsg", [128, 8, 512]); b_wsg = Buf()
                cwb = sbt(s2, "cwb", [128, 3, 512]); b_cwb = Buf()
                cbT = sbt(s2, "cbT", [128, 12]); b_cbT = Buf()
                Wk = sbt(s2, "Wk", [128, 3, 8, 512], BF16); b_Wk = Buf()
                xo = [sbt(s2, "x2o%d" % i, [128, 512], BF16) for i in range(2)]; b_xo = bufs(2)
                T.dma(wsg[:], win_r[:, :, 1024:1536], writes=[b_wsg])
                T.dma(cwb[:], D["conv_w"][:, 1024:1536].partition_broadcast(128), writes=[b_cwb])
                T.dma(cbT[:], D["conv_bT"][:, :], writes=[b_cbT])
                for k in range(3):
                    T.op("dve", lambda e, k=k: e.tensor_tensor(out=Wk[:, k, :, :], in0=wsg[:], in1=cwb[:, k:k + 1, :].to_broadcast([128, 8, 512]), op=ALU.mult),
                         reads=[b_wsg, b_cwb], writes=[b_Wk])
                n_it = 0
                for cc in range(4):
                    for tb in range(4):
                        ps, bps = psum()
                        first = True
                        for k in range(3):
                            for ch in range(8):
                                c0 = tb * 512 + 127 + k
                                mm(ps[:], Wk[:, k, ch, cc * 128:(cc + 1) * 128], hTe[:, ch, c0:c0 + 512], first, (k == 2 and ch == 7),
                                   [b_Wk, b_hTe[ch]], [bps])
                                first = False
                        o_, bo = xo[n_it % 2], b_xo[n_it % 2]
                        n_it += 1
                        evac(o_[:], ps[:], [bps, b_cbT], [bo], bias=cbT[:, 8 + cc:9 + cc])
                        T.dma(D["s_x2T"][cc * 128:(cc + 1) * 128, tb * 512:(tb + 1) * 512], o_[:], reads=[bo], writes=[SB["s_x2T"][cc * 4 + tb]])

            qT = sbt(sa, "qT", [64, 8, OWN], BF16); b_qT = bufs(8)
            kT = sbt(sa, "kT", [64, 2, EXT], BF16); b_kT = bufs(2)
            kcT = sbt(sa, "kcT", [64, 2, 256], BF16); b_kcT = Buf()
            vaug = sbt(sa, "vaug", [128, 20, 2, 64], BF16); b_vaug = bufs(20)
            with Phase(T) as s3:
                wst3 = sbt(s3, "wst3", [128, 8, 768]); b_wst3 = Buf()
                wqk = sbt(s3, "wqk", [128, 8, 768], BF16); b_wqk = Buf()
                wqkp = sbt(s3, "wqkp", [128, 8, 640], BF16); b_wqkp = Buf()
                rc = sbt(s3, "rc", [64, EXT]); b_rc = Buf()
                rs = sbt(s3, "rs", [64, EXT]); b_rs = Buf()
                t1 = [sbt(s3, "rt1_%d" % i, [64, 512]) for i in range(2)]; b_t1 = bufs(2)
                t2 = [sbt(s3, "rt2_%d" % i, [64, 512]) for i in range(2)]; b_t2 = bufs(2)
                T.dma(rc[:], D["rope_c"][:, :], writes=[b_rc])
                T.dma(rs[:], D["rope_s"][:, :], writes=[b_rs])
                T.dma(wst3[:], win_r[:, :, 1536:2304], writes=[b_wst3])
                T.op("dve", lambda e: e.tensor_copy(out=wqk[:], in_=wst3[:]), reads=[b_wst3], writes=[b_wqk])
                T.dma(wst3[:, :, 0:640], D["w_in_perm"].rearrange("(ch p) n -> p ch n", p=128), reads=[], writes=[b_wst3])
                T.op("dve", lambda e: e.tensor_copy(out=wqkp[:], in_=wst3[:, :, 0:640]), reads=[b_wst3], writes=[b_wqkp])
                n_it = 0
                for hd in range(10):
                    if hd < 8:
                        blocks = [(128 + i * 512, 512) for i in range(4)]
                    else:
                        blocks = [(i * 512, 512) for i in range(4)] + [(2048, 256)]
                    for (e0, nb) in blocks:
                        ps1, bp1 = psum()
                        ps2, bp2 = psum()
                        for ch in range(8):
                            mm(ps1[0:64, 0:nb], wqk[:, ch, hd * 64:(hd + 1) * 64], hTe[:, ch, e0:e0 + nb], ch == 0, ch == 7, [b_wqk, b_hTe[ch]], [bp1])
                        for ch in range(8):
                            mm(ps2[0:64, 0:nb], wqkp[:, ch, hd * 64:(hd + 1) * 64], hTe[:, ch, e0:e0 + nb], ch == 0, ch == 7, [b_wqkp, b_hTe[ch]], [bp2])
                        a_, ba = t1[n_it % 2], b_t1[n_it % 2]
                        c_, bc = t2[n_it % 2], b_t2[n_it % 2]
                        n_it += 1
                        T.op("dve", lambda e, a_=a_, ps1=ps1, e0=e0, nb=nb: e.tensor_tensor(out=a_[:, 0:nb], in0=ps1[0:64, 0:nb], in1=rc[:, e0:e0 + nb], op=ALU.mult),
                             reads=[bp1, b_rc], writes=[ba])
                        T.op("dve", lambda e, c_=c_, ps2=ps2, e0=e0, nb=nb: e.tensor_tensor(out=c_[:, 0:nb], in0=ps2[0:64, 0:nb], in1=rs[:, e0:e0 + nb], op=ALU.mult),
                             reads=[bp2, b_rs], writes=[bc])
                        if hd < 8:
                            dst, bd = qT[:, hd, e0 - 128:e0 - 128 + nb], b_qT[hd]
                        else:
                            dst, bd = kT[:, hd - 8, e0:e0 + nb], b_kT[hd - 8]
                        T.op("pool", lambda e, dst=dst, a_=a_, c_=c_, nb=nb: e.tensor_tensor(out=dst, in0=a_[:, 0:nb], in1=c_[:, 0:nb], op=ALU.add),
                             reads=[ba, bc], writes=[bd])
                for g in range(2):
                    ps1, bp1 = psum()
                    for ch in range(8):
                        mm(ps1[0:64, 0:256], wqk[:, ch, 512 + g * 64:512 + (g + 1) * 64], hcT[:, ch, :], ch == 0, ch == 7, [b_wqk, b_hcT], [bp1])
                    evac(kcT[:, g, :], ps1[0:64, 0:256], [bp1], [b_kcT])
                for tl in range(20):
                    ps1, bp1 = psum()
                    for ch in range(8):
                        lt = hTe[:, ch, tl * 128:(tl + 1) * 128] if tl < 18 else hcT[:, ch, (tl - 18) * 128:(tl - 17) * 128]
                        mm(ps1[:, 0:128], lt, wqk[:, ch, 640:768], ch == 0, ch == 7, [b_wqk, b_hcT] + b_hTe, [bp1])
                    evac(vaug[:, tl, :, :], ps1[:, 0:128].rearrange("p (g d) -> p g d", g=2), [bp1], [b_vaug[tl]])

            dbg_dump("d_qT", qT[:].rearrange("p a b -> p (a b)"), b_qT[7])
            dbg_dump("d_kT", kT[:].rearrange("p a b -> p (a b)"), b_kT[1])

            with Phase(T) as s4:
                mprev = sbt(s4, "mprev", [128, 16, 128], BF16); b_mprev = Buf()
                mnext = sbt(s4, "mnext", [128, 16, 128], BF16); b_mnext = Buf()
                esk = sbt(s4, "esk", [64, 8]); b_esk = Buf()
                E = [[sbt(s4, "E%d_%d" % (i, j), [128, 512], BF16) for j in range(5)] for i in range(2)]
                b_E = [bufs(5) for _ in range(2)]
                zt = [sbt(s4, "zt%d" % i, [64, 512]) for i in range(2)]; b_zt = bufs(2)
                ao = [sbt(s4, "ao%d" % i, [64, 512], BF16) for i in range(2)]; b_ao = bufs(2)
                T.dma(mprev[:], D["mask_prev"].rearrange("p (a b) -> p a b", b=128), writes=[b_mprev])
                T.dma(mnext[:], D["mask_next"].rearrange("p (a b) -> p a b", b=128), writes=[b_mnext])
                T.dma(esk[:], D["sink_row"][0:1, :].partition_broadcast(64), writes=[b_esk])
                T.op("act", lambda e: e.activation(out=esk[:], in_=esk[:], func=AF.Exp), reads=[b_esk], writes=[b_esk])
                it = 0
                for qb in range(16):
                    for g in range(2):
                        Eb, bE = E[it % 2], b_E[it % 2]
                        z_, bz = zt[it % 2], b_zt[it % 2]
                        a_, ba = ao[it % 2], b_ao[it % 2]
                        it += 1
                        rhs_q = qT[:, 4 * g:4 * g + 4, qb * 128:(qb + 1) * 128]
                        rq = b_qT[4 * g:4 * g + 4]
                        for kb in range(5):
                            if kb < 3:
                                lt, rl = kT[:, g, (qb + kb) * 128:(qb + kb + 1) * 128], [b_kT[g]]
                            else:
                                lt, rl = kcT[:, g, (kb - 3) * 128:(kb - 2) * 128], [b_kcT]
                            ps, bps = psum()
                            mm(ps[:].rearrange("p (a b) -> p a b", a=4), lt, rhs_q, True, True, rl + rq, [bps])
                            T.op("act", lambda e, ps=ps, o=Eb[kb]: e.activation(out=o[:], in_=ps[:], func=AF.Exp, scale=0.125), reads=[bps], writes=[bE[kb]])
                            if kb == 0 or kb == 2:
                                mk, bmk = (mprev, b_mprev) if kb == 0 else (mnext, b_mnext)
                                T.op("pool", lambda e, o=Eb[kb], mk=mk, qb=qb: e.tensor_tensor(
                                    out=o[:].rearrange("p (a b) -> p a b", a=4), in0=o[:].rearrange("p (a b) -> p a b", a=4),
                                    in1=mk[:, qb:qb + 1, :].to_broadcast([128, 4, 128]), op=ALU.mult), reads=[bE[kb], bmk], writes=[bE[kb]])
                        pso, bpo = psum()
                        psz, bpz = psum()
                        for kb in range(5):
                            vt = (qb + kb) if kb < 3 else (18 + kb - 3)
                            mm(pso[0:64, :], vaug[:, vt, g, :], Eb[kb][:], kb == 0, kb == 4, [b_vaug[vt], bE[kb]], [bpo])
                        for kb in range(5):
                            mm(psz[0:64, :], onesb[:, 0:64], Eb[kb][:], kb == 0, kb == 4, [b_onesb, bE[kb]], [bpz])
                        T.op("dve", lambda e, z_=z_, psz=psz, g=g: e.tensor_tensor(
                            out=z_[:].rearrange("p (a b) -> p a b", a=4), in0=psz[0:64, :].rearrange("p (a b) -> p a b", a=4),
                            in1=esk[:, 4 * g:4 * g + 4].unsqueeze(2).to_broadcast([64, 4, 128]), op=ALU.add), reads=[bpz, b_esk], writes=[bz])
                        T.op("dve", lambda e, z_=z_: e.reciprocal(out=z_[:], in_=z_[:]), reads=[bz], writes=[bz])
                        T.op("dve", lambda e, a_=a_, pso=pso, z_=z_: e.tensor_tensor(out=a_[:], in0=pso[0:64, :], in1=z_[:], op=ALU.mult),
                             reads=[bpo, bz], writes=[ba])
                        dst = D["s_attT"].rearrange("p (h n) -> p h n", h=8)[:, 4 * g:4 * g + 4, qb * 128:(qb + 1) * 128]
                        T.dma(dst, a_[:].rearrange("p (a b) -> p a b", a=4), reads=[ba], writes=[SB["s_attT"][qb * 2 + g]])

        if "d_attT" in dbgn:
            T.dma(D["d_attT"], D["s_attT"], reads=SB["s_attT"], writes=[Buf()])
        if "d_x2T" in dbgn:
            T.dma(D["d_x2T"], D["s_x2T"], reads=SB["s_x2T"], writes=[Buf()])
        if stage == "pa":
            T.drain()
            return nc

        with Phase(T) as sbk:
            v_tm = sbt(sbk, "v_tm", [128, 32, 512], BF16); b_v = bufs(32)
            x1_tm = sbt(sbk, "x1_tm", [128, 32, 512], BF16); b_x1 = bufs(32)
            h2f = sbt(sbk, "h2f", [64, SEQ], BF16); b_h2f = Buf()
            with Phase(T) as s1:
                hTf = sbt(s1, "hTf", [128, 8, SEQ + 2], BF16); b_hTf2 = [bufs(2) for _ in range(8)]
                T.op("pool", lambda e: e.memset(hTf[:, :, 0:1], 0.0), writes=[b_hTf2[ch][0] for ch in range(8)])
                T.op("pool", lambda e: e.memset(hTf[:, :, SEQ + 1:SEQ + 2], 0.0), writes=[b_hTf2[ch][1] for ch in range(8)])

                def hTf_deps(ch, tl):
                    if tl < 15:
                        return [b_hTf2[ch][0]]
                    if tl > 16:
                        return [b_hTf2[ch][1]]
                    return b_hTf2[ch]
                with Phase(T) as s1a:
                    xst = [sbt(s1a, "xsf%d" % i, [128, 2048]) for i in range(2)]; b_xst = bufs(2)
                    xtf = D["xT_full"].rearrange("(ch p) n -> p ch n", p=128)
                    for hf in range(2):
                        for ch in range(8):
                            x_, bx = xst[ch % 2], b_xst[ch % 2]
                            T.dma(x_[:], xtf[:, ch, hf * 2048:(hf + 1) * 2048], writes=[bx])
                            evac(hTf[:, ch, 1 + hf * 2048:1 + (hf + 1) * 2048], x_[:], [bx, b_ops1, b_modT], [b_hTf2[ch][hf]],
                                 scale=ops1[:, ch, 0:1], bias=modT[:, ch, 0:1])
                with Phase(T) as s1b:
                    wsg = sbt(s1b, "wsgf", [128, 8, 512]); b_wsg = Buf()
                    cwb = sbt(s1b, "cwbf", [128, 3, 512]); b_cwb = Buf()
                    cbr = sbt(s1b, "cbr", [1, 512]); b_cbr = Buf()
                    cbb = sbt(s1b, "cbb", [1, 512], BF16); b_cbb = Buf()
                    Wk = sbt(s1b, "Wkf", [128, 3, 8, 512], BF16); b_Wk = Buf()
                    for cb in range(2):
                        dst, bdst = (v_tm, b_v) if cb == 0 else (x1_tm, b_x1)
                        T.dma(wsg[:], win_r[:, :, cb * 512:(cb + 1) * 512], writes=[b_wsg])
                        T.dma(cwb[:], D["conv_w"][:, cb * 512:(cb + 1) * 512].partition_broadcast(128), writes=[b_cwb])
                        T.dma(cbr[:], D["conv_b"][0:1, cb * 512:(cb + 1) * 512], writes=[b_cbr])
                        T.op("dve", lambda e: e.tensor_copy(out=cbb[:], in_=cbr[:]), reads=[b_cbr], writes=[b_cbb])
                        for k in range(3):
                            T.op("dve", lambda e, k=k: e.tensor_tensor(out=Wk[:, k, :, :], in0=wsg[:], in1=cwb[:, k:k + 1, :].to_broadcast([128, 8, 512]), op=ALU.mult),
                                 reads=[b_wsg, b_cwb], writes=[b_Wk])
                        for tl in range(32):
                            ps, bps = psum()
                            first = True
                            for k in range(3):
                                for ch in range(8):
                                    mm(ps[:], hTf[:, ch, tl * 128 + k:tl * 128 + k + 128], Wk[:, k, ch, :], first, False, [b_Wk] + hTf_deps(ch, tl), [bps])
                                    first = False
                            mm(ps[:], onesb[0:1, 0:128], cbb[0:1, :], False, True, [b_onesb, b_cbb], [bps])
                            evac(dst[:, tl, :], ps[:], [bps], [bdst[tl]])

            if "d_v" in dbgn:
                T.dma(D["d_v"].rearrange("(t p) c -> p t c", p=128), v_tm[:], reads=b_v, writes=[Buf()])
                T.dma(D["d_x1"].rearrange("(t p) c -> p t c", p=128), x1_tm[:], reads=b_x1, writes=[Buf()])

            with Phase(T) as s2:
                zT = sbt(s2, "zT_sb", [17, SEQ]); b_zT = Buf()
                w1 = sbt(s2, "fw1", [17, 64]); w2 = sbt(s2, "fw2", [64, 64]); b_w12 = Buf()
                pr = sbt(s2, "fpr", [64, 6]); b_pr = Buf()
                h1f = sbt(s2, "h1f", [64, SEQ]); b_h1f = Buf()
                ar = [sbt(s2, "far%d" % i, [64, 512]) for i in range(2)]; b_ar = bufs(2)
                kr = [sbt(s2, "fkr%d" % i, [64, 512]) for i in range(2)]; b_kr = bufs(2)
                T.dma(zT[:], D["zT"][:, :], writes=[b_zT])
                T.dma(w1[:], D["f_w1"][:, :], writes=[b_w12])
                T.dma(w2[:], D["f_w2"][:, :], writes=[b_w12])
                for i, nm in enumerate(["f_b1", "f_f1", "f_b2", "f_f2"]):
                    T.dma(pr[:, i:i + 1], D[nm][:, :], writes=[b_pr])
                T.op("dve", lambda e: e.tensor_tensor(out=pr[:, 4:5], in0=pr[:, 0:1], in1=pr[:, 1:2], op=ALU.mult), reads=[b_pr], writes=[b_pr])
                T.op("dve", lambda e: e.tensor_tensor(out=pr[:, 5:6], in0=pr[:, 2:3], in1=pr[:, 3:4], op=ALU.mult), reads=[b_pr], writes=[b_pr])
                it = 0
                for layer in range(2):
                    for blk in range(8):
                        ps, bps = psum()
                        if layer == 0:
                            mm(ps[0:64, :], w1[:], zT[:, blk * 512:(blk + 1) * 512], True, True, [b_w12, b_zT], [bps])
                            fcol, fbcol = pr[:, 1:2], pr[:, 4:5]
                        else:
                            mm(ps[0:64, :], w2[:], h1f[:, blk * 512:(blk + 1) * 512], True, True, [b_w12, b_h1f], [bps])
                            fcol, fbcol = pr[:, 3:4], pr[:, 5:6]
                        a_, ba = ar[it % 2], b_ar[it % 2]
                        k_, bk = kr[it % 2], b_kr[it % 2]
                        it += 1
                        T.op("dve", lambda e, a_=a_, ps=ps, fcol=fcol, fbcol=fbcol: e.tensor_scalar(out=a_[:], in0=ps[0:64, :], scalar1=fcol, scalar2=fbcol, op0=ALU.mult, op1=ALU.add),
                             reads=[bps, b_pr], writes=[ba])
                        T.op("dve", lambda e, a_=a_, k_=k_: e.tensor_scalar(out=k_[:], in0=a_[:], scalar1=1.0 / (2 * math.pi), scalar2=MAGIC, op0=ALU.mult, op1=ALU.add),
                             reads=[ba], writes=[bk])
                        T.op("dve", lambda e, k_=k_: e.tensor_scalar(out=k_[:], in0=k_[:], scalar1=MAGIC, scalar2=-2 * math.pi, op0=ALU.subtract, op1=ALU.mult),
                             reads=[bk], writes=[bk])
                        T.op("dve", lambda e, a_=a_, k_=k_: e.tensor_tensor(out=a_[:], in0=a_[:], in1=k_[:], op=ALU.add), reads=[ba, bk], writes=[ba])
                        if layer == 0:
                            T.op("act", lambda e, a_=a_, blk=blk: e.activation(out=h1f[:, blk * 512:(blk + 1) * 512], in_=a_[:], func=AF.Sin), reads=[ba], writes=[b_h1f])
                        else:
                            T.op("act", lambda e, a_=a_, blk=blk: e.activation(out=h2f[:, blk * 512:(blk + 1) * 512], in_=a_[:], func=AF.Sin), reads=[ba], writes=[b_h2f])

            tabs_c = [sbt(sbk, "tabc%d" % i, [128, 32, 128], BF16) for i in range(2)]; b_tc = bufs(2)
            tabs_s = [sbt(sbk, "tabs%d" % i, [128, 32, 128], BF16) for i in range(2)]; b_ts = bufs(2)
            dc_r = D["dft_c"].rearrange("o p (k j) -> o p k j", j=128)
            ds_r = D["dft_s"].rearrange("o p (k j) -> o p k j", j=128)
            tabn = [0]

            def load_tabs(oc):
                i = tabn[0] % 2
                tabn[0] += 1
                T.dma(tabs_c[i][:], dc_r[oc], writes=[b_tc[i]])
                T.dma(tabs_s[i][:], ds_r[oc], writes=[b_ts[i]])
                return tabs_c[i], b_tc[i], tabs_s[i], b_ts[i]

            nyr = sbt(sbk, "nyr", [1, 512]); b_nyr = Buf()
            altr = sbt(sbk, "altr", [1, 512], BF16); b_altr = Buf()
            T.dma(altr[:], D["altrow"][:, 0:512], writes=[b_altr])

            def nyq_row(src, b_src):
                i = tabn[0] % 2
                tabn[0] += 1
                T.dma(tabs_s[i][:], D["nyq_fwd"].rearrange("p (k j) -> p k j", j=128), writes=[b_ts[i]])
                psx, bpx = psum()
                for kc in range(32):
                    mm(psx[:], tabs_s[i][:, kc, :], src[:, kc, :], kc == 0, kc == 31, [b_ts[i], b_src[kc]], [bpx])
                T.op("act", lambda e: e.copy(out=nyr[:], in_=psx[0:1, :]), reads=[bpx], writes=[b_nyr])

            spec_r = D["s_spec"]
            with Phase(T) as s3:
                w3b = sbt(s3, "w3b", [64, 2048], BF16); b_w3b = Buf()
                b3b = sbt(s3, "b3b", [1, 2048], BF16); b_b3b = Buf()
                skp = sbt(s3, "skp", [1, 2, 512]); b_skp = Buf()
                with Phase(T) as s3a:
                    w3s = sbt(s3a, "w3s", [64, 2048]); b_w3s = Buf()
                    b3s = sbt(s3a, "b3s", [1, 2048]); b_b3s = Buf()
                    T.dma(w3s[:], D["f_w3"][:, :], writes=[b_w3s])
                    T.dma(b3s[:], D["f_b3"][:, :], writes=[b_b3s])
                    T.op("dve", lambda e: e.tensor_tensor(out=w3b[:, 0:1024], in0=w3s[:, 0:1024], in1=w3s[:, 1024:2048], op=ALU.add), reads=[b_w3s], writes=[b_w3b])
                    T.op("dve", lambda e: e.tensor_tensor(out=w3b[:, 1024:2048], in0=w3s[:, 0:1024], in1=w3s[:, 1024:2048], op=ALU.subtract), reads=[b_w3s], writes=[b_w3b])
                    T.op("dve", lambda e: e.tensor_tensor(out=b3b[:, 0:1024], in0=b3s[:, 0:1024], in1=b3s[:, 1024:2048], op=ALU.add), reads=[b_b3s], writes=[b_b3b])
                    T.op("dve", lambda e: e.tensor_tensor(out=b3b[:, 1024:2048], in0=b3s[:, 0:1024], in1=b3s[:, 1024:2048], op=ALU.subtract), reads=[b_b3s], writes=[b_b3b])
                hs = sbt(s3, "hs", [128, 32, 512], BF16); b_hs = bufs(32)
                hd = sbt(s3, "hd", [128, 32, 512], BF16); b_hd = bufs(32)
                wn = [sbt(s3, "wn%d" % i, [128, 512]) for i in range(2)]; b_wn = bufs(2)
                tA = [sbt(s3, "tA%d" % i, [128, 512]) for i in range(2)]; b_tA = bufs(2)
                tB = [sbt(s3, "tB%d" % i, [128, 512]) for i in range(1)]; b_tB = bufs(1)
                spt = [sbt(s3, "spt%d" % i, [128, 2, 512]) for i in range(1)] * 2; b_spt = bufs(1) * 2
                T.dma(skp[:], D["hy_skip"].rearrange("(a o) c -> a o c", a=1), writes=[b_skp])
                for o in range(2):
                    for tc in range(32):
                        w_, bw = wn[tc % 2], b_wn[tc % 2]
                        T.dma(w_[:], D["window"][tc * 128:(tc + 1) * 128, :], writes=[bw])
                        psf, bpf = psum()
                        psb, bpb = psum()
                        cf, cb_ = o * 512, 1024 + o * 512
                        mm(psf[:], h2f[:, tc * 128:(tc + 1) * 128], w3b[:, cf:cf + 512], True, False, [b_h2f, b_w3b], [bpf])
                        mm(psf[:], onesb[0:1, 0:128], b3b[0:1, cf:cf + 512], False, True, [b_onesb, b_b3b], [bpf])
                        mm(psb[:], h2f[:, tc * 128:(tc + 1) * 128], w3b[:, cb_:cb_ + 512], True, False, [b_h2f, b_w3b], [bpb])
                        mm(psb[:], onesb[0:1, 0:128], b3b[0:1, cb_:cb_ + 512], False, True, [b_onesb, b_b3b], [bpb])
                        if tc > 0:
                            T.op("dve", lambda e, psf=psf, w_=w_, tc=tc: e.tensor_tensor(out=hs[:, tc, :], in0=psf[:], in1=w_[:], op=ALU.mult), reads=[bpf, bw], writes=[b_hs[tc]])
                            T.op("dve", lambda e, psb=psb, w_=w_, tc=tc: e.tensor_tensor(out=hd[:, tc, :], in0=psb[:], in1=w_[:], op=ALU.mult), reads=[bpb, bw], writes=[b_hd[tc]])
                        else:
                            A_, bA = tA[0], b_tA[0]
                            B_, bB = tB[0], b_tB[0]
                            r0, br0 = tA[1], b_tA[1]
                            T.op("dve", lambda e, A_=A_, psf=psf, w_=w_: e.tensor_tensor(out=A_[:], in0=psf[:], in1=w_[:], op=ALU.mult), reads=[bpf, bw], writes=[bA])
                            T.op("dve", lambda e, B_=B_, psb=psb, w_=w_: e.tensor_tensor(out=B_[:], in0=psb[:], in1=w_[:], op=ALU.mult), reads=[bpb, bw], writes=[bB])
                            T.op("dve", lambda e, A_=A_, B_=B_, r0=r0: e.tensor_tensor(out=r0[0:1, :], in0=A_[0:1, :], in1=B_[0:1, :], op=ALU.subtract), reads=[bA, bB], writes=[br0])
                            T.op("dve", lambda e, A_=A_, r0=r0: e.scalar_tensor_tensor(out=A_[0:1, :], in0=r0[0:1, :], scalar=-0.5, in1=A_[0:1, :], op0=ALU.mult, op1=ALU.add), reads=[br0, bA], writes=[bA])
                            T.op("dve", lambda e, B_=B_, r0=r0: e.scalar_tensor_tensor(out=B_[0:1, :], in0=r0[0:1, :], scalar=0.5, in1=B_[0:1, :], op0=ALU.mult, op1=ALU.add), reads=[br0, bB], writes=[bB])
                            T.op("dve", lambda e, A_=A_, o=o: e.tensor_tensor(out=A_[0:1, :], in0=A_[0:1, :], in1=skp[0:1, o, :], op=ALU.add), reads=[bA, b_skp], writes=[bA])
                            T.op("dve", lambda e, B_=B_, o=o: e.tensor_tensor(out=B_[0:1, :], in0=B_[0:1, :], in1=skp[0:1, o, :], op=ALU.add), reads=[bB, b_skp], writes=[bB])
                            T.op("dve", lambda e, A_=A_: e.tensor_copy(out=hs[:, 0, :], in_=A_[:]), reads=[bA], writes=[b_hs[0]])
                            T.op("dve", lambda e, B_=B_: e.tensor_copy(out=hd[:, 0, :], in_=B_[:]), reads=[bB], writes=[b_hd[0]])
                    if "d_hs" in dbgn and o == 0:
                        T.dma(D["d_hs"].rearrange("(t p) c -> p t c", p=128), hs[:], reads=b_hs, writes=[Buf()])
                    nyq_row(hs, b_hs)
                    nxt = load_tabs(0)
                    for fc in range(32):
                        tcb, btc, tsb, bts = nxt
                        if fc + 1 < 32:
                            nxt = load_tabs(fc + 1)
                        psr, bpr = psum()
                        psi, bpi = psum()
                        for kc in range(32):
                            mm(psr[:], tcb[:, kc, :], hs[:, kc, :], kc == 0, kc == 31, [btc, b_hs[kc]], [bpr])
                        for kc in range(32):
                            mm(psi[:], tsb[:, kc, :], hd[:, kc, :], kc == 0, kc == 31, [bts, b_hd[kc]], [bpi])
                        sp, bsp = spt[fc % 2], b_spt[fc % 2]
                        T.op("act", lambda e, sp=sp, psr=psr: e.activation(out=sp[:, 0, :], in_=psr[:], func=AF.Copy, scale=2.0 / NFFT), reads=[bpr], writes=[bsp])
                        T.op("act", lambda e, sp=sp, psi=psi: e.activation(out=sp[:, 1, :], in_=psi[:], func=AF.Copy, scale=2.0 / NFFT), reads=[bpi], writes=[bsp])
                        if fc == 0:
                            T.op("act", lambda e, sp=sp: e.activation(out=sp[0:1, 1, :], in_=nyr[:], func=AF.Copy, scale=2.0 / NFFT), reads=[b_nyr, bsp], writes=[bsp])
                        T.dma(spec_r[o, fc].rearrange("p (a c) -> p a c", a=2), sp[:], reads=[bsp], writes=[SB["s_spec"][o * 32 + fc]])
            if "d_spec" in dbgn:
                T.dma(D["d_spec"], D["s_spec"], reads=SB["s_spec"], writes=[Buf()])

            with Phase(T) as s4:
                Y = sbt(s4, "Y", [128, 32, 2, 512], BF16); b_Y = bufs(32)
                stl = [sbt(s4, "stl%d" % i, [128, 2, 512]) for i in range(2)]; b_stl = bufs(2)
                tt = [sbt(s4, "tt%d" % i, [128, 512]) for i in range(4)]; b_tt = bufs(4)

                def forward(src, b_src, o):
                    nyq_row(src, b_src)
                    nxt = load_tabs(0)
                    for fc in range(32):
                        tcb, btc, tsb, bts = nxt
                        if fc + 1 < 32:
                            nxt = load_tabs(fc + 1)
                        S_, bS = stl[fc % 2], b_stl[fc % 2]
                        T.dma(S_[:], spec_r[o, fc].rearrange("p (a c) -> p a c", a=2), reads=[SB["s_spec"][o * 32 + fc]], writes=[bS])
                        psr, bpr = psum()
                        psi, bpi = psum()
                        for kc in range(32):
                            mm(psr[:], tcb[:, kc, :], src[:, kc, :], kc == 0, kc == 31, [btc, b_src[kc]], [bpr])
                        for kc in range(32):
                            mm(psi[:], tsb[:, kc, :], src[:, kc, :], kc == 0, kc == 31, [bts, b_src[kc]], [bpi])
                        T.op("dve", lambda e, psr=psr, S_=S_: e.tensor_tensor(out=tt[0][:], in0=psr[:], in1=S_[:, 0, :], op=ALU.mult), reads=[bpr, bS], writes=[b_tt[0]])
                        T.op("dve", lambda e, psi=psi, S_=S_: e.tensor_tensor(out=tt[1][:], in0=psi[:], in1=S_[:, 1, :], op=ALU.mult), reads=[bpi, bS], writes=[b_tt[1]])
                        T.op("dve", lambda e, psr=psr, S_=S_: e.tensor_tensor(out=tt[2][:], in0=psr[:], in1=S_[:, 1, :], op=ALU.mult), reads=[bpr, bS], writes=[b_tt[2]])
                        T.op("dve", lambda e, psi=psi, S_=S_: e.tensor_tensor(out=tt[3][:], in0=psi[:], in1=S_[:, 0, :], op=ALU.mult), reads=[bpi, bS], writes=[b_tt[3]])
                        T.op("pool", lambda e, fc=fc: e.tensor_tensor(out=Y[:, fc, 0, :], in0=tt[0][:], in1=tt[1][:], op=ALU.subtract), reads=[b_tt[0], b_tt[1]], writes=[b_Y[fc]])
                        T.op("pool", lambda e, fc=fc: e.tensor_tensor(out=Y[:, fc, 1, :], in0=tt[2][:], in1=tt[3][:], op=ALU.add), reads=[b_tt[2], b_tt[3]], writes=[b_Y[fc]])
                        if fc == 0:
                            T.op("dve", lambda e, psr=psr, S_=S_: e.scalar_tensor_tensor(out=Y[0:1, 0, 0, :], in0=psr[0:1, :], scalar=0.5, in1=S_[0:1, 0, :], op0=ALU.mult, op1=ALU.mult),
                                 reads=[bpr, bS, b_Y[0]], writes=[b_Y[0]])
                            T.op("dve", lambda e, psi=psi, S_=S_: e.scalar_tensor_tensor(out=Y[0:1, 0, 1, :], in0=nyr[:], scalar=0.5, in1=S_[0:1, 1, :], op0=ALU.mult, op1=ALU.mult),
                                 reads=[b_nyr, bS, b_Y[0]], writes=[b_Y[0]])

                forward(v_tm, b_v, 0)
                nxt = load_tabs(0)
                for oc in range(32):
                    tcb, btc, tsb, bts = nxt
                    if oc + 1 < 32:
                        nxt = load_tabs(oc + 1)
                    ps, bps = psum()
                    for kc in range(32):
                        mm(ps[:], tcb[:, kc, :], Y[:, kc, 0, :], kc == 0, False, [btc, b_Y[kc]], [bps])
                    for kc in range(32):
                        mm(ps[:], tsb[:, kc, :], Y[:, kc, 1, :], False, False, [bts, b_Y[kc]], [bps])
                    mm(ps[:], altr[0:1, 0:128], Y[0:1, 0, 1, :], False, True, [b_altr, b_Y[0]], [bps])
                    T.op("dve", lambda e, ps=ps, oc=oc: e.tensor_tensor(out=x1_tm[:, oc, :], in0=ps[:], in1=x1_tm[:, oc, :], op=ALU.mult), reads=[bps, b_x1[oc]], writes=[b_x1[oc]])
                if "d_y1" in dbgn:
                    T.dma(D["d_y1"].rearrange("(t p) c -> p t c", p=128), x1_tm[:], reads=b_x1, writes=[Buf()])
                forward(x1_tm, b_x1, 1)
                toc = v_tm[:].rearrange("p a b -> p (a b)").rearrange("p (k j) -> p k j", j=512)
                tos = x1_tm[:].rearrange("p a b -> p (a b)").rearrange("p (k j) -> p k j", j=512)
                x2l = [sbt(s4, "x2l%d" % i, [128, 512], BF16) for i in range(2)]; b_x2l = bufs(2)
                hyo = [sbt(s4, "hyo%d" % i, [128, 512], BF16) for i in range(2)]; b_hyo = bufs(2)
                it = 0
                for tb in range(4):
                    T.dma(toc, D["dfto_c"][tb].rearrange("p (k j) -> p k j", j=512), reads=[], writes=b_v)
                    T.dma(tos, D["dfto_s"][tb].rearrange("p (k j) -> p k j", j=512), reads=[], writes=b_x1)
                    for cc in range(4):
                        xl, bxl = x2l[it % 2], b_x2l[it % 2]
                        ho, bho = hyo[it % 2], b_hyo[it % 2]
                        it += 1
                        T.dma(xl[:], D["s_x2T"][cc * 128:(cc + 1) * 128, tb * 512:(tb + 1) * 512], reads=[SB["s_x2T"][cc * 4 + tb]], writes=[bxl])
                        ps, bps = psum()
                        for kc in range(32):
                            mm(ps[:], Y[:, kc, 0, cc * 128:(cc + 1) * 128], toc[:, kc, :], kc == 0, False, [b_v[0], b_Y[kc]], [bps])
                        for kc in range(32):
                            mm(ps[:], Y[:, kc, 1, cc * 128:(cc + 1) * 128], tos[:, kc, :], False, False, [b_x1[0], b_Y[kc]], [bps])
                        mm(ps[:], Y[0:1, 0, 1, cc * 128:(cc + 1) * 128], altr[0:1, 0:512], False, True, [b_altr, b_Y[0]], [bps])
                        T.op("dve", lambda e, ps=ps, xl=xl, ho=ho: e.tensor_tensor(out=ho[:], in0=ps[:], in1=xl[:], op=ALU.mult), reads=[bps, bxl], writes=[bho])
                        T.dma(D["s_hyT"][cc * 128:(cc + 1) * 128, tb * 512:(tb + 1) * 512], ho[:], reads=[bho], writes=[SB["s_hyT"][cc * 4 + tb]])
        if "d_hyT" in dbgn:
            T.dma(D["d_hyT"], D["s_hyT"], reads=SB["s_hyT"], writes=[Buf()])
        if stage == "pb":
            T.drain()
            return nc

        h2T = sbt(st, "h2T", [128, 8, OWN], BF16); b_h2T = bufs(16)
        lng = sbt(st, "lng", [128, DM]); lnb = sbt(st, "lnb", [128, DM]); b_ln = Buf()

        def layer_norm_tile(tt, btt, scr, bscr, mv, bmv):
            T.op("dve", lambda e: e.bn_stats(out=scr[:, 0, :], in_=tt[:, 0:512]), reads=[btt], writes=[bscr])
            T.op("dve", lambda e: e.bn_stats(out=scr[:, 1, :], in_=tt[:, 512:1024]), reads=[btt], writes=[bscr])
            T.op("dve", lambda e: e.bn_aggr(out=mv[:, 0:2], in_=scr[:].rearrange("p a b -> p (a b)")), reads=[bscr], writes=[bmv])
            T.op("act", lambda e: e.activation(out=mv[:, 2:3], in_=mv[:, 1:2], func=AF.Sqrt, bias=epsc[:, 0:1]), reads=[bmv, b_epsc], writes=[bmv])
            T.op("dve", lambda e: e.reciprocal(out=mv[:, 2:3], in_=mv[:, 2:3]), reads=[bmv], writes=[bmv])
            T.op("dve", lambda e: e.tensor_scalar(out=tt[:], in0=tt[:], scalar1=mv[:, 0:1], scalar2=mv[:, 2:3], op0=ALU.subtract, op1=ALU.mult),
                 reads=[btt, bmv], writes=[btt])
            T.op("pool", lambda e: e.tensor_tensor(out=tt[:], in0=tt[:], in1=lng[:], op=ALU.mult), reads=[btt, b_ln], writes=[btt])
            T.op("pool", lambda e: e.tensor_tensor(out=tt[:], in0=tt[:], in1=lnb[:], op=ALU.add), reads=[btt, b_ln], writes=[btt])

        sq_r = D["s_qT"].rearrange("p (k n) -> p k n", k=16)
        with Phase(T) as sq:
            wqs = sbt(sq, "wqs", [128, 8, 512]); b_wqs = Buf()
            wqb = sbt(sq, "wqb", [128, 8, 2048], BF16); b_wqb = bufs(4)
            wq_r = D["peer_wq"].rearrange("(ch p) n -> p ch n", p=128)
            for cb in range(4):
                T.dma(wqs[:], wq_r[:, :, cb * 512:(cb + 1) * 512], writes=[b_wqs])
                T.op("pool", lambda e, cb=cb: e.tensor_copy(out=wqb[:, :, cb * 512:(cb + 1) * 512], in_=wqs[:]), reads=[b_wqs], writes=[b_wqb[cb]])
            with Phase(T) as sc_:
                hyT = sbt(sc_, "hyT", [128, 4, OWN], BF16); b_hyT = Buf()
                attT = sbt(sc_, "attT", [64, 8, OWN], BF16); b_attT = Buf()
                wo_hy = sbt(sc_, "wo_hy", [128, 4, DM], BF16); wo_att = sbt(sc_, "wo_att", [64, 8, DM], BF16); b_wo = Buf()
                T.dma(lng[:], D["ln1_g"][0:1, :].partition_broadcast(128), writes=[b_ln])
                T.dma(lnb[:], D["ln1_b"][0:1, :].partition_broadcast(128), writes=[b_ln])
                T.dma(hyT[:], D["s_hyT"].rearrange("(c p) n -> p c n", p=128), reads=SB["s_hyT"], writes=[b_hyT])
                T.dma(attT[:], D["s_attT"].rearrange("p (h n) -> p h n", h=8), reads=SB["s_attT"], writes=[b_attT])
                with Phase(T) as sc1:
                    wos = sbt(sc1, "wos", [128, 4, DM]); b_wos = Buf()
                    T.dma(wos[:], D["w_out"][0:512, :].rearrange("(c p) d -> p c d", p=128), writes=[b_wos])
                    T.op("dve", lambda e: e.tensor_copy(out=wo_hy[:], in_=wos[:]), reads=[b_wos], writes=[b_wo])
                    wos2 = wos[:].rearrange("p a b -> p (a b)")[0:64, :].rearrange("p (h d) -> p h d", h=4)
                    for hh in range(2):
                        T.dma(wos2, D["w_out"][512 + hh * 256:512 + (hh + 1) * 256, :].rearrange("(h p) d -> p h d", p=64), writes=[b_wos])
                        T.op("dve", lambda e, hh=hh: e.tensor_copy(out=wo_att[:, hh * 4:(hh + 1) * 4, :], in_=wos2), reads=[b_wos], writes=[b_wo])
                xo = [sbt(sc_, "xo%d" % i, [128, DM]) for i in range(2)]; b_xo2 = bufs(2)
                tt = [sbt(sc_, "ttc%d" % i, [128, DM]) for i in range(2)]; b_ttc = bufs(2)
                scrs = [sbt(sc_, "lnscr%d" % i, [128, 2, 6]) for i in range(2)]; b_scrs = bufs(2)
                mvs = [sbt(sc_, "lnmv%d" % i, [128, 4]) for i in range(2)]; b_mvs = bufs(2)

                def pc_mm(tl):
                    res = []
                    for hf in range(2):
                        ps, bps = psum()
                        for cc in range(4):
                            mm(ps[:], hyT[:, cc, tl * 128:(tl + 1) * 128], wo_hy[:, cc, hf * 512:(hf + 1) * 512], cc == 0, False, [b_hyT, b_wo], [bps])
                        for h in range(8):
                            mm(ps[:], attT[:, h, tl * 128:(tl + 1) * 128], wo_att[:, h, hf * 512:(hf + 1) * 512], False, h == 7, [b_attT, b_wo], [bps])
                        res.append((ps, bps))
                    return res

                nxt_mm = pc_mm(0)
                T.dma(xo[0][:], D["x_own"][0:128, :], writes=[b_xo2[0]])
                for tl in range(16):
                    cur_mm = nxt_mm
                    x_, bx = xo[tl % 2], b_xo2[tl % 2]
                    t_, bt = tt[tl % 2], b_ttc[tl % 2]
                    if tl + 1 < 16:
                        T.dma(xo[(tl + 1) % 2][:], D["x_own"][(tl + 1) * 128:(tl + 2) * 128, :], writes=[b_xo2[(tl + 1) % 2]])
                    for hf in range(2):
                        ps, bps = cur_mm[hf]
                        T.op("dve", lambda e, t_=t_, ps=ps, hf=hf: e.tensor_tensor(out=t_[:, hf * 512:(hf + 1) * 512], in0=ps[:], in1=g1b[:, hf * 512:(hf + 1) * 512], op=ALU.mult),
                             reads=[bps, b_g1b], writes=[bt])
                    T.op("dve", lambda e, t_=t_, x_=x_: e.scalar_tensor_tensor(out=t_[:], in0=x_[:], scalar=ALPHA, in1=t_[:], op0=ALU.mult, op1=ALU.add), reads=[bx, bt], writes=[bt])
                    if tl + 1 < 16:
                        nxt_mm = pc_mm(tl + 1)
                    layer_norm_tile(t_, bt, scrs[tl % 2], b_scrs[tl % 2], mvs[tl % 2], b_mvs[tl % 2])
                    T.dma(D["s_x1"][tl * 128:(tl + 1) * 128, :], t_[:], reads=[bt], writes=[SB["s_x1"][tl]])
                    for half in range(2):
                        ps, bps = psum()
                        for j in range(4):
                            ch = half * 4 + j
                            T.op("pe", lambda e, ps=ps, j=j, ch=ch, t_=t_: e.transpose(ps[:, j * 128:(j + 1) * 128], t_[:, ch * 128:(ch + 1) * 128], ident[:]),
                                 reads=[bt, b_ident], writes=[bps])
                        for j in range(4):
                            ch = half * 4 + j
                            evac(h2T[:, ch, tl * 128:(tl + 1) * 128], ps[:, j * 128:(j + 1) * 128], [bps, b_ops1, b_modT], [b_h2T[tl]],
                                 scale=ops1[:, 8 + ch, 0:1], bias=modT[:, 24 + ch, 0:1])
            qo = [sbt(sq, "qo%d" % i, [128, 512], BF16) for i in range(2)]; b_qo = bufs(2)
            it = 0
            for ck in range(16):
                for tb in range(4):
                    ps, bps = psum()
                    for ch in range(8):
                        mm(ps[:], wqb[:, ch, ck * 128:(ck + 1) * 128], h2T[:, ch, tb * 512:(tb + 1) * 512], ch == 0, ch == 7,
                           [b_wqb[ck // 4]] + b_h2T[tb * 4:(tb + 1) * 4], [bps])
                    q_, bq = qo[it % 2], b_qo[it % 2]
                    it += 1
                    evac(q_[:], ps[:], [bps], [bq])
                    T.dma(sq_r[:, ck, tb * 512:(tb + 1) * 512], q_[:], reads=[bq], writes=[SB["s_qT"][ck * 4 + tb]])

        if "d_x1o" in dbgn:
            T.dma(D["d_x1o"], D["s_x1"], reads=SB["s_x1"], writes=[Buf()])
        if "d_h2T" in dbgn:
            T.dma(D["d_h2T"].rearrange("(c p) n -> p c n", p=128), h2T[:], reads=b_h2T, writes=[Buf()])
        if stage == "pc":
            T.drain()
            return nc

        with Phase(T) as sd:
            k1b = sbt(sd, "k1b", [128, 128], BF16); k2b = sbt(sd, "k2b", [128, 128], BF16); b_kb = Buf()
            K2rep = sbt(sd, "K2rep", [128, 4, 128], BF16); b_K2rep = Buf()
            thr_all = sbt(sd, "thr_all", [128, 16, 8]); bF_all = sbt(sd, "bF_all", [128, 16, 8]); b_thr = bufs(16)
            T.dma(lng[:], D["ln2_g"][0:1, :].partition_broadcast(128), writes=[b_ln])
            T.dma(lnb[:], D["ln2_b"][0:1, :].partition_broadcast(128), writes=[b_ln])
            with Phase(T) as sd1:
                kst = sbt(sd1, "kst", [128, 2, 128]); b_kst = Buf()
                T.dma(kst[:, 0, :], D["keys1T"][:, :], writes=[b_kst])
                T.dma(kst[:, 1, :], D["keys2T"][:, :], writes=[b_kst])
                T.op("dve", lambda e: e.tensor_copy(out=k1b[:], in_=kst[:, 0, :]), reads=[b_kst], writes=[b_kb])
                T.op("dve", lambda e: e.tensor_copy(out=k2b[:], in_=kst[:, 1, :]), reads=[b_kst], writes=[b_kb])
                T.op("dve", lambda e: e.tensor_copy(out=K2rep[:], in_=kst[:, 1:2, :].to_broadcast([128, 4, 128])), reads=[b_kst], writes=[b_K2rep])
            for pss in range(2):
                with Phase(T) as sp_:
                    qTh = sbt(sp_, "qTh", [128, 16, 1024], BF16); b_qTh = Buf()
                    acc = sbt(sp_, "acc", [128, 8, DM]); b_acc = bufs(8)
                    T.dma(qTh[:], sq_r[:, :, pss * 1024:(pss + 1) * 1024], reads=SB["s_qT"], writes=[b_qTh])
                    T.op("pool", lambda e: e.memset(acc[:], 0.0), writes=b_acc)
                    with Phase(T) as sp1:
                        m16 = sbt(sp1, "m16", [128, 16, 16]); b_m16s = bufs(16)
                        tmpk = sbt(sp1, "tmpk", [128, 16, 128]); b_tmpk = bufs(16)
                        tmp2k = sbt(sp1, "tmp2k", [128, 8, 256]); b_tmp2k = bufs(8)
                        b_c16s = bufs(8)
                        tmp = sbt(sp1, "tk_tmp", [128, 128]); b_tmp = Buf()
                        cand = sbt(sp1, "cand", [128, 8, 256]); b_cand = Buf()
                        tmp2 = sbt(sp1, "tk_tmp2", [128, 256]); b_tmp2 = Buf()
                        c16 = sbt(sp1, "c16", [128, 8, 16]); b_c16 = Buf()
                        d16 = sbt(sp1, "d16", [128, 8, 16]); b_d16 = Buf()
                        zz = sbt(sp1, "zz", [128, 8]); b_zz = Buf()
                        for t in range(8):
                            gt = pss * 8 + t
                            banks = [psum() for _ in range(4)]
                            for ck in range(16):
                                ps, bps = banks[ck // 4]
                                mm(ps[:, (ck % 4) * 128:(ck % 4 + 1) * 128], qTh[:, ck, t * 128:(t + 1) * 128], (k1b if ck % 2 == 0 else k2b)[:], True, True,
                                   [b_qTh, b_kb], [bps])
                            srcs = []
                            for ck in range(16):
                                ps, bps = banks[ck // 4]
                                srcs.append((ps[:, (ck % 4) * 128:(ck % 4 + 1) * 128], bps))
                            for ck in range(16):
                                src, bps = srcs[ck]
                                T.op("dve", lambda e, ck=ck, src=src: e.max(out=m16[:, ck, 0:8], in_=src), reads=[bps], writes=[b_m16s[ck]])
                            for ck in range(16):
                                src, bps = srcs[ck]
                                T.op("dve", lambda e, ck=ck, src=src: e.match_replace(out=tmpk[:, ck, :], in_to_replace=m16[:, ck, 0:8], in_values=src, imm_value=-1e30),
                                     reads=[bps, b_m16s[ck]], writes=[b_tmpk[ck]])
                            for ck in range(16):
                                T.op("dve", lambda e, ck=ck: e.max(out=m16[:, ck, 8:16], in_=tmpk[:, ck, :]), reads=[b_tmpk[ck]], writes=[b_m16s[ck]])
                            m16v = m16[:].rearrange("p (h two) k -> p h two k", two=2)
                            T.op("dve", lambda e, m16v=m16v: e.tensor_tensor(
                                out=cand[:].rearrange("p h (a b) -> p h a b", a=16),
                                in0=m16v[:, :, 0, :].unsqueeze(3).to_broadcast([128, 8, 16, 16]),
                                in1=m16v[:, :, 1, :].unsqueeze(2).to_broadcast([128, 8, 16, 16]), op=ALU.add), reads=b_m16s, writes=[b_cand])
                            for h in range(8):
                                T.op("dve", lambda e, h=h: e.max(out=c16[:, h, 0:8], in_=cand[:, h, :]), reads=[b_cand], writes=[b_c16s[h]])
                            for h in range(8):
                                T.op("dve", lambda e, h=h: e.match_replace(out=tmp2k[:, h, :], in_to_replace=c16[:, h, 0:8], in_values=cand[:, h, :], imm_value=-1e30),
                                     reads=[b_cand, b_c16s[h]], writes=[b_tmp2k[h]])
                            for h in range(8):
                                T.op("dve", lambda e, h=h: e.max(out=c16[:, h, 8:16], in_=tmp2k[:, h, :]), reads=[b_tmp2k[h]], writes=[b_c16s[h]])
                            T.op("dve", lambda e: e.tensor_tensor(out=d16[:], in0=c16[:], in1=c16[:, :, 0:1].to_broadcast([128, 8, 16]), op=ALU.subtract),
                                 reads=b_c16s, writes=[b_d16])
                            T.op("act", lambda e: e.activation(out=d16[:], in_=d16[:], func=AF.Exp), reads=[b_d16], writes=[b_d16])
                            T.op("dve", lambda e: e.tensor_reduce(out=zz[:], in_=d16[:], axis=AX.X, op=ALU.add), reads=[b_d16], writes=[b_zz])
                            T.op("act", lambda e: e.activation(out=zz[:], in_=zz[:], func=AF.Ln), reads=[b_zz], writes=[b_zz])
                            T.op("dve", lambda e: e.tensor_tensor(out=zz[:], in0=zz[:], in1=c16[:, :, 0], op=ALU.add), reads=[b_zz] + b_c16s, writes=[b_zz])
                            T.op("dve", lambda e, gt=gt: e.tensor_scalar(out=bF_all[:, gt, :], in0=zz[:], scalar1=-1.0, scalar2=None, op0=ALU.mult), reads=[b_zz], writes=[b_thr[gt]])
                            T.op("dve", lambda e, gt=gt: e.tensor_scalar(out=thr_all[:, gt, :], in0=c16[:, :, 15], scalar1=-1e-5, scalar2=None, op0=ALU.add), reads=b_c16s, writes=[b_thr[gt]])

                    with Phase(T) as sp3:
                        stg = sbt(sp3, "stg", [128, 8, 512]); b_stg = Buf()
                        stg_v = stg[:].rearrange("p a b -> p (a b)").rearrange("p (c d) -> p c d", c=4)
                        UTb = [sbt(sp3, "UTb%d" % i, [128, 8, 512], BF16) for i in range(2)]; b_UTb = bufs(2)
                        Vb = [sbt(sp3, "Vb%d" % i, [128, 4, DM], BF16) for i in range(2)]; b_Vb = bufs(2)
                        K1s = [sbt(sp3, "K1s%d" % i, [128, 4, 128], BF16) for i in range(2)]; b_K1s = bufs(2)
                        Ag = [sbt(sp3, "Ag%d" % i, [128, 512]) for i in range(2)]; b_Ag = bufs(2)
                        AGT = [sbt(sp3, "AGT%d" % i, [128, 512], BF16) for i in range(2)]; b_AGT = bufs(2)
                        Hs = sbt(sp3, "Hs", [128, 8, 512], BF16); b_Hs = bufs(8)
                        As = [sbt(sp3, "As%d" % i, [128, 8, 512], BF16) for i in range(2)]; b_As = bufs(2)
                        EF3 = [sbt(sp3, "EFx%d" % i, [128, 512], BF16) for i in range(3)]; b_EF3 = bufs(3)
                        Gm4 = [sbt(sp3, "Gmx%d" % i, [128, 512], BF16) for i in range(4)]; b_Gm4 = bufs(4)
                        uT_r = D["peer_uT"].rearrange("(ch p) n -> p ch n", p=128)
                        v_r = D["peer_v"].rearrange("(g c p) d -> g p c d", c=4, p=128)
                        K2flat = K2rep[:].rearrange("p a b -> p (a b)")

                        def load_U(g):
                            i = g % 2
                            T.dma(stg[:], uT_r[:, :, g * 512:(g + 1) * 512], writes=[b_stg])
                            T.op("pool", lambda e: e.tensor_copy(out=UTb[i][:], in_=stg[:]), reads=[b_stg], writes=[b_UTb[i]])

                        def load_V(g):
                            i = g % 2
                            T.dma(stg_v, v_r[g], writes=[b_stg])
                            T.op("pool", lambda e: e.tensor_copy(out=Vb[i][:], in_=stg_v), reads=[b_stg], writes=[b_Vb[i]])
                            T.op("pool", lambda e: e.tensor_copy(out=K1s[i][:], in_=k1b[:, 4 * g:4 * g + 4].unsqueeze(2).to_broadcast([128, 4, 128])), reads=[b_kb], writes=[b_K1s[i]])

                        dcnt = [0]
                        dbank = {}

                        def emit_D(g, t, h):
                            bk = (1, 2, 3, 0)[dcnt[0] % 4]
                            dcnt[0] += 1
                            dbank[(g, t, h)] = bk
                            psD, bpD = PSB[bk], PSBUF[bk]
                            mm(psD[:], qTh[:, 2 * h + 1, t * 128:(t + 1) * 128], K2flat, True, False, [b_qTh, b_K2rep], [bpD])
                            mm(psD[:], qTh[:, 2 * h, t * 128:(t + 1) * 128], K1s[g % 2][:].rearrange("p a b -> p (a b)"), False, True, [b_qTh, b_K1s[g % 2]], [bpD])

                        def make_tail(g, t, itn):
                            gi = g % 2
                            ag, bag = Ag[itn % 2], b_Ag[itn % 2]
                            agt, bagt = AGT[itn % 2], b_AGT[itn % 2]
                            psT, bpT = PSB[4], PSBUF[4]

                            def s_tr():
                                for c in range(4):
                                    T.op("pe", lambda e, c=c: e.transpose(psT[:, c * 128:(c + 1) * 128], ag[:, c * 128:(c + 1) * 128], ident[:]),
                                         reads=[bag, b_ident], writes=[bpT])

                            def s_cp():
                                T.op("act", lambda e: e.copy(out=agt[:], in_=psT[:]), reads=[bpT], writes=[bagt])

                            def s_v(hf):
                                def f():
                                    psO, bpO = PSB[6 + hf], PSBUF[6 + hf]
                                    for c in range(4):
                                        mm(psO[:], agt[:, c * 128:(c + 1) * 128], Vb[gi][:, c, hf * 512:(hf + 1) * 512], c == 0, c == 3, [bagt, b_Vb[gi]], [bpO])
                                return f

                            def s_acc(hf):
                                def f():
                                    psO, bpO = PSB[6 + hf], PSBUF[6 + hf]
                                    T.op("dve", lambda e: e.tensor_tensor(out=acc[:, t, hf * 512:(hf + 1) * 512], in0=acc[:, t, hf * 512:(hf + 1) * 512],
                                                                          in1=psO[:], op=ALU.add), reads=[bpO, b_acc[t]], writes=[b_acc[t]])
                                return f
                            return {0: [s_tr], 2: [s_cp], 3: [s_v(0)], 4: [s_v(1)], 5: [s_acc(0)], 6: [s_acc(1)]}

                        def gelu_group(g):
                            T.op("act", lambda e: e.activation(out=As[g % 2][:], in_=Hs[:], func=AF.Gelu), reads=b_Hs, writes=[b_As[g % 2]])

                        load_U(0)
                        for t in range(8):
                            gt = pss * 8 + t
                            psH, bpH = PSB[0], PSBUF[0]
                            for ch in range(8):
                                mm(psH[:], h2T[:, ch, gt * 128:(gt + 1) * 128], UTb[0][:, ch, :], ch == 0, ch == 7, [b_h2T[gt], b_UTb[0]], [bpH])
                            T.op("act", lambda e, t=t, psH=psH: e.copy(out=Hs[:, t, :], in_=psH[:]), reads=[bpH], writes=[b_Hs[t]])
                        gelu_group(0)
                        load_U(1)
                        load_V(0)
                        pending = {}
                        itn = 0
                        efn = 0
                        for g in range(32):
                            gi = g % 2
                            for t in range(8):
                                gt = pss * 8 + t
                                for h in range(4):
                                    emit_D(g, t, h)
                                psH, bpH = PSB[4], PSBUF[4]
                                psG, bpG = PSB[5], PSBUF[5]
                                for h in range(8):
                                    bk = dbank[(g, t, h)]
                                    psD, bpD = PSB[bk], PSBUF[bk]
                                    ef, bef = EF3[efn % 3], b_EF3[efn % 3]
                                    gm, bgm = Gm4[efn % 4], b_Gm4[efn % 4]
                                    efn += 1
                                    T.op("act", lambda e, ef=ef, psD=psD, gt=gt, h=h: e.activation(out=ef[:], in_=psD[:], func=AF.Exp, bias=bF_all[:, gt, h:h + 1]),
                                         reads=[bpD, b_thr[gt]], writes=[bef])
                                    T.op("dve", lambda e, gm=gm, psD=psD, ef=ef, gt=gt, h=h: e.scalar_tensor_tensor(out=gm[:], in0=psD[:], scalar=thr_all[:, gt, h:h + 1], in1=ef[:],
                                                                                                                      op0=ALU.is_ge, op1=ALU.mult), reads=[bpD, bef, b_thr[gt]], writes=[bgm])
                                    for f in pending.get(h, []):
                                        f()
                                    if h == 1 and t == 1:
                                        if g + 1 < 32:
                                            load_V(g + 1)
                                        if g + 2 < 32:
                                            load_U(g + 2)
                                    if g + 1 < 32 and h >= 4:
                                        for hh in (2 * (h - 4), 2 * (h - 4) + 1):
                                            mm(psH[:], h2T[:, hh, gt * 128:(gt + 1) * 128], UTb[(g + 1) % 2][:, hh, :], hh == 0, hh == 7, [b_h2T[gt], b_UTb[(g + 1) % 2]], [bpH])
                                    if h + 4 < 8:
                                        emit_D(g, t, h + 4)
                                    mm(psG[:], identb[:], gm[:], h == 0, h == 7, [b_identb, bgm], [bpG])
                                if g + 1 < 32:
                                    T.op("act", lambda e, t=t, psH=psH: e.copy(out=Hs[:, t, :], in_=psH[:]), reads=[bpH], writes=[b_Hs[t]])
                                ag, bag = Ag[itn % 2], b_Ag[itn % 2]
                                T.op("dve", lambda e, ag=ag, psG=psG, gi=gi, t=t: e.tensor_tensor(out=ag[:], in0=As[gi][:, t, :], in1=psG[:], op=ALU.mult),
                                     reads=[b_As[gi], bpG], writes=[bag])
                                pending = make_tail(g, t, itn)
                                itn += 1
                            if g + 1 < 32:
                                gelu_group(g + 1)
                        for h in range(8):
                            for f in pending.get(h, []):
                                f()
                    if "d_pm" in dbgn:
                        T.dma(D["d_pm"][pss * 1024:(pss + 1) * 1024, :].rearrange("(t p) d -> p t d", p=128), acc[:], reads=b_acc, writes=[Buf()])
                    with Phase(T) as sp2:
                        x1l = [sbt(sp2, "x1l%d" % i, [128, DM]) for i in range(2)]; b_x1l = bufs(2)
                        scr = sbt(sp2, "lnscr2", [128, 2, 6]); b_scr = Buf()
                        mv = sbt(sp2, "lnmv2", [128, 4]); b_mv = Buf()
                        T.dma(x1l[0][:], D["s_x1"][pss * 1024:pss * 1024 + 128, :], reads=[SB["s_x1"][pss * 8]], writes=[b_x1l[0]])
                        for t in range(8):
                            gt = pss * 8 + t
                            x_, bx = x1l[t % 2], b_x1l[t % 2]
                            if t + 1 < 8:
                                T.dma(x1l[(t + 1) % 2][:], D["s_x1"][(gt + 1) * 128:(gt + 2) * 128, :], reads=[SB["s_x1"][gt + 1]], writes=[b_x1l[(t + 1) % 2]])
                            T.op("dve", lambda e, t=t: e.tensor_tensor(out=acc[:, t, :], in0=acc[:, t, :], in1=g2b[:], op=ALU.mult), reads=[b_acc[t], b_g2b], writes=[b_acc[t]])
                            T.op("dve", lambda e, t=t, x_=x_: e.scalar_tensor_tensor(out=acc[:, t, :], in0=x_[:], scalar=ALPHA, in1=acc[:, t, :], op0=ALU.mult, op1=ALU.add),
                                 reads=[bx, b_acc[t]], writes=[b_acc[t]])
                            layer_norm_tile(acc[:, t, :], b_acc[t], scr, b_scr, mv, b_mv)
                            T.dma(D["out"][gt * 128:(gt + 1) * 128, :], acc[:, t, :], reads=[b_acc[t]], writes=[Buf()])
            if "d_thr" in dbgn:
                T.dma(D["d_thr"], thr_all[:].rearrange("p a b -> p (a b)"), reads=b_thr, writes=[Buf()])
                T.dma(D["d_bF"], bF_all[:].rearrange("p a b -> p (a b)"), reads=b_thr, writes=[Buf()])
        T.drain()
    return nc


_CONST = {}


def _bf(a):
    return np.ascontiguousarray(a.astype(ml_dtypes.bfloat16))


def _constants():
    if _CONST:
        return _CONST
    n = SEQ
    t = np.linspace(0.0, 1.0, n, dtype=np.float32)[:, None]
    w = (2.0 * math.pi * np.arange(n, dtype=np.float32)[:, None] / n).astype(np.float32)
    bands = np.linspace(1e-4, 7, 8, dtype=np.float32)[None, :]
    z = np.concatenate([t, np.cos(bands * w), -np.sin(bands * w)], axis=-1).astype(np.float32)
    _CONST["zT"] = np.ascontiguousarray(z.T)
    min_decay = math.log(1e-2) / 1.5
    max_decay = math.log(1e-2) / 0.3
    deltas = np.abs(np.linspace(min_decay, max_decay, 512, dtype=np.float32))
    _CONST["window"] = (np.exp(-t * deltas[None, :]) + np.float32(0.05)).astype(np.float32)
    a = np.arange(4096, dtype=np.int64)
    ab = (a[:, None] * a[None, :]) % NFFT
    ang = ab.astype(np.float64) * (2.0 * math.pi / NFFT)
    Mc = np.cos(ang).astype(np.float32)
    Ms = np.sin(ang).astype(np.float32)
    alt = np.where(a % 2 == 0, 1.0, -1.0).astype(np.float32)
    nyq = np.zeros((128, 32, 128), np.float32)
    nyq[:, :, 0] = alt.reshape(32, 128).T
    _CONST["nyq_fwd"] = _bf(nyq.reshape(128, 4096))
    _CONST["altrow"] = _bf(alt[None, :])
    del ang, ab
    for nm, M in (("c", Mc), ("s", Ms)):
        Mb = M.astype(ml_dtypes.bfloat16)
        T1 = Mb.reshape(32, 128, 32, 128).transpose(2, 1, 0, 3)
        _CONST["dft_" + nm] = np.ascontiguousarray(T1).reshape(32, 128, 4096)
        for half in range(2):
            sub = Mb[:, half * 2048:(half + 1) * 2048].reshape(32, 128, 4, 512).transpose(2, 1, 0, 3)
            _CONST["dfto_%s%d" % (nm, half)] = np.ascontiguousarray(sub).reshape(4, 128, 16384)
    inv = (10000.0 ** (-np.arange(16, dtype=np.float32) / 16)).astype(np.float32)
    for half in range(2):
        pos = half * OWN - 128 + np.arange(EXT)
        valid = ((pos >= 0) & (pos < SEQ)).astype(np.float32)
        posc = np.clip(pos, 0, SEQ - 1)
        row = (posc // 64).astype(np.float32)
        col = (posc % 64).astype(np.float32)
        ar = row[None, :] * inv[:, None]
        ac = col[None, :] * inv[:, None]
        rc = np.concatenate([np.cos(ar), np.cos(ar), np.cos(ac), np.cos(ac)], axis=0).astype(np.float32)
        rs = np.concatenate([-np.sin(ar), np.sin(ar), -np.sin(ac), np.sin(ac)], axis=0).astype(np.float32)
        _CONST["rope_c%d" % half] = np.ascontiguousarray(rc)
        _CONST["rope_s%d" % half] = np.ascontiguousarray(rs)
        v2 = np.concatenate([valid[0:128], valid[EXT - 128:EXT]])[None, :].repeat(128, axis=0)
        _CONST["validT%d" % half] = np.ascontiguousarray(v2.astype(np.float32))
        kj = np.arange(128)[:, None]
        qi = np.arange(128)[None, :]
        mp = np.zeros((128, 16, 128), np.float32)
        mn = np.zeros((128, 16, 128), np.float32)
        for qb in range(16):
            gb = half * 16 + qb
            if gb - 1 >= 0:
                mp[:, qb, :] = (kj >= qi)
            if gb + 1 < 32:
                mn[:, qb, :] = (kj <= qi)
        _CONST["mask_prev%d" % half] = _bf(mp.reshape(128, 2048))
        _CONST["mask_next%d" % half] = _bf(mn.reshape(128, 2048))
    return _CONST


def _colT(v, ncols):
    return np.ascontiguousarray(np.asarray(v, np.float32).reshape(ncols, 128).T)


def prep_inputs(inputs, cores=range(N_CORES)):
    C = _constants()
    g = {k: np.asarray(v) for k, v in inputs.items()}
    perm = np.concatenate([np.arange(16, 32), np.arange(0, 16), np.arange(48, 64), np.arange(32, 48)])
    w_in = np.ascontiguousarray(g["w_in"][0])
    cols = (1536 + (np.arange(10)[:, None] * 64 + perm[None, :])).reshape(-1)
    shared = {
        "ctxT": None, "cctxT": _colT(g["c_ctx"], 8),
        "w_mod": np.ascontiguousarray(g["w_mod"][0]), "b_modT": _colT(g["b_mod"][0], 48),
        "b_mod_row": np.ascontiguousarray(g["b_mod"][0][None, :]),
        "w_in": w_in, "w_in_perm": np.ascontiguousarray(w_in[:, cols]),
        "conv_w": np.ascontiguousarray(g["hy_conv_w"][0]), "conv_b": np.ascontiguousarray(g["hy_conv_b"][0][None, :]),
        "conv_bT": _colT(g["hy_conv_b"][0], 12),
        "f_w1": np.ascontiguousarray(g["hy_f_w1"][0]), "f_b1": np.ascontiguousarray(g["hy_f_b1"][0][:, None]),
        "f_f1": np.ascontiguousarray(g["hy_f_freq1"][0][:, None]),
        "f_w2": np.ascontiguousarray(g["hy_f_w2"][0]), "f_b2": np.ascontiguousarray(g["hy_f_b2"][0][:, None]),
        "f_f2": np.ascontiguousarray(g["hy_f_freq2"][0][:, None]),
        "f_w3": np.ascontiguousarray(g["hy_f_w3"][0]), "f_b3": np.ascontiguousarray(g["hy_f_b3"][0][None, :]),
        "hy_skip": np.ascontiguousarray(g["hy_skip"][0]),
        "zT": C["zT"], "window": C["window"], "nyq_fwd": C["nyq_fwd"], "altrow": C["altrow"], "dft_c": C["dft_c"], "dft_s": C["dft_s"],
        "sink_row": np.ascontiguousarray(g["attn_sink"][0][None, :]),
        "w_out": np.ascontiguousarray(g["w_out"][0]),
        "ln1_g": np.ascontiguousarray(g["ln1_g"]), "ln1_b": np.ascontiguousarray(g["ln1_b"]),
        "ln2_g": np.ascontiguousarray(g["ln2_g"]), "ln2_b": np.ascontiguousarray(g["ln2_b"]),
        "peer_wq": np.ascontiguousarray(g["peer_wq"][0]),
        "keys1T": np.ascontiguousarray(g["peer_keys1"][0].T), "keys2T": np.ascontiguousarray(g["peer_keys2"][0].T),
        "peer_uT": np.ascontiguousarray(g["peer_u"][0].T), "peer_v": np.ascontiguousarray(g["peer_v"][0]),
    }
    maps = []
    xT_cache = {}
    for core in cores:
        b, half = core // 2, core % 2
        if b not in xT_cache:
            xT_cache[b] = np.ascontiguousarray(g["x"][b].T)
        xT = xT_cache[b]
        s0 = half * OWN - 128
        xe = np.zeros((DM, EXT), np.float32)
        lo, hi = max(s0, 0), min(s0 + EXT, SEQ)
        xe[:, lo - s0:hi - s0] = xT[:, lo:hi]
        m = dict(shared)
        m.update({
            "xT_full": xT, "xT_ext": xe, "x_own": np.ascontiguousarray(g["x"][b, half * OWN:(half + 1) * OWN]),
            "ctxT": np.ascontiguousarray(g["ctx"][b].T), "cT": _colT(g["c"][b], 8),
            "dfto_c": C["dfto_c%d" % half], "dfto_s": C["dfto_s%d" % half],
            "rope_c": C["rope_c%d" % half], "rope_s": C["rope_s%d" % half],
            "mask_prev": C["mask_prev%d" % half], "mask_next": C["mask_next%d" % half], "validT": C["validT%d" % half],
        })
        maps.append(m)
    return maps


_NC = {}


def kernel(**inputs):
    if "nc" not in _NC:
        _NC["nc"] = build("full")
    maps = prep_inputs(inputs)
    res = run_bass_kernel_spmd(_NC["nc"], maps, core_ids=list(range(N_CORES)))
    out = np.zeros((4, SEQ, DM), np.float32)
    for core in range(N_CORES):
        b, half = core // 2, core % 2
        out[b, half * OWN:(half + 1) * OWN] = np.asarray(res.results[core]["out"], np.float32)
    return out
```
